# Optimizing a Trainium2 kernel written in Bass

```python
import jax
import jax.numpy as jnp
from jax import lax
import numpy as np

D_MODEL = 1024
BATCH = 16
SEQ = 2048
DEPTH = 4

N_META = 16
POOL_WIDTH = D_MODEL // 2
POOL_WINDOWS = (2, 4, 8, 16)
N_POOL_GROUPS = len(POOL_WINDOWS)
POOL_GROUP = POOL_WIDTH // N_POOL_GROUPS
RWKV_WIDTH = D_MODEL - POOL_WIDTH
HEAD_SIZE = 64
N_RWKV_HEADS = RWKV_WIDTH // HEAD_SIZE
DECAY_LORA = 64
AAA_LORA = 64
MV_LORA = 32
GATE_LORA = 160
D_FF = -(-8 * D_MODEL // (3 * 256)) * 256
RMS_EPS = 1e-6
GN_EPS = 64e-5
L2_EPS = 1e-12

SHIFT_WIDTHS = (RWKV_WIDTH, RWKV_WIDTH, RWKV_WIDTH, DECAY_LORA, AAA_LORA, GATE_LORA)
N_SHIFT = sum(SHIFT_WIDTHS)
N_IN = POOL_WIDTH + N_SHIFT

kernel_name = "hymba_pool_rwkv7_hybrid"


def rms_norm(x, g):
    xf = x.astype(jnp.float32)
    y = xf * lax.rsqrt(jnp.mean(xf * xf, axis=-1, keepdims=True) + RMS_EPS)
    return (y * g.astype(jnp.float32)).astype(x.dtype)


def split_cols(p, widths):
    idx = np.cumsum(widths)[:-1].tolist()
    return jnp.split(p, idx, axis=-1)


def token_shift(p, mu):
    prev = jnp.pad(p, ((0, 0), (1, 0), (0, 0)))[:, :-1]
    return p + (prev - p) * mu


def to_heads(z):
    return z.reshape(z.shape[:-1] + (N_RWKV_HEADS, HEAD_SIZE))


def multiscale_pool(u, w_grp, scale):
    T = u.shape[1]
    uf = u.astype(jnp.float32)
    cs = jnp.pad(jnp.cumsum(uf, axis=1), ((0, 0), (1, 0), (0, 0)))
    pos = jnp.arange(T)
    outs = []
    for g, win in enumerate(POOL_WINDOWS):
        c = cs[..., g * POOL_GROUP:(g + 1) * POOL_GROUP]
        upper = c[:, 1:]
        lower = jnp.pad(c[:, :T + 1 - win], ((0, 0), (win - 1, 0), (0, 0)))
        count = jnp.minimum(pos + 1, win).astype(jnp.float32)[None, :, None]
        outs.append((upper - lower) / count)
    pooled = jnp.stack(outs, axis=2)
    diff = pooled - uf.reshape(pooled.shape)
    y = jnp.einsum('btgc,gcd->btgd', diff, w_grp.astype(jnp.float32))
    return (y.reshape(u.shape) * scale.astype(jnp.float32)).astype(u.dtype)


def rwkv7_scan(r, w, k, v, a, b):
    bsz = r.shape[0]

    def step(S, inp):
        r_t, w_t, k_t, v_t, a_t, b_t = inp
        sa = jnp.einsum('bhvk,bhk->bhv', S, a_t)
        S = (S * w_t[:, :, None, :] + sa[..., None] * b_t[:, :, None, :]
             + v_t[..., None] * k_t[:, :, None, :])
        y = jnp.einsum('bhvk,bhk->bhv', S, r_t)
        return S, y

    xs = tuple(jnp.moveaxis(z, 1, 0) for z in (r, w, k, v, a, b))
    S0 = jnp.zeros((bsz, N_RWKV_HEADS, HEAD_SIZE, HEAD_SIZE), jnp.float32)
    _, ys = lax.scan(step, S0, xs)
    return jnp.moveaxis(ys, 0, 1)


def setup_inputs(seed: int = 0) -> dict:
    key = jax.random.key(seed)
    ks = iter(jax.random.split(key, 32))

    def nrm(shape, scale):
        return jax.random.normal(next(ks), shape, jnp.float32) * scale

    def unif(shape, lo, hi):
        return jax.random.uniform(next(ks), shape, jnp.float32, lo, hi)

    L, Lv = DEPTH, DEPTH - 1
    return {
        "x": nrm((BATCH, SEQ, D_MODEL), 1.0),
        "meta_tokens": nrm((N_META, D_MODEL), 1.0),
        "ln1_g": 1.0 + nrm((L, D_MODEL), 0.05),
        "w_in": nrm((L, D_MODEL, N_IN), D_MODEL ** -0.5),
        "mu_shift": unif((L, N_SHIFT), 0.0, 1.0),
        "pool_w": nrm((L, N_POOL_GROUPS, POOL_GROUP, POOL_GROUP), POOL_GROUP ** -0.5),
        "pool_scale": 1.0 + nrm((L, POOL_WIDTH), 0.1),
        "w0": unif((L, RWKV_WIDTH), -6.0, -1.0),
        "w_lora_up": nrm((L, DECAY_LORA, RWKV_WIDTH), 0.5 * DECAY_LORA ** -0.5),
        "a0": nrm((L, RWKV_WIDTH), 0.1),
        "a_lora_up": nrm((L, AAA_LORA, RWKV_WIDTH), AAA_LORA ** -0.5),
        "g_lora_up": nrm((L, GATE_LORA, RWKV_WIDTH), GATE_LORA ** -0.5),
        "k_k": 0.85 + nrm((L, RWKV_WIDTH), 0.05),
        "k_a": 1.0 + nrm((L, RWKV_WIDTH), 0.05),
        "r_k": nrm((L, RWKV_WIDTH), 0.1),
        "gn_w": 1.0 + nrm((L, RWKV_WIDTH), 0.05),
        "gn_b": nrm((L, RWKV_WIDTH), 0.01),
        "w_in_vres": nrm((Lv, D_MODEL, MV_LORA), D_MODEL ** -0.5),
        "mu_vres": unif((Lv, MV_LORA), 0.0, 1.0),
        "v0": nrm((Lv, RWKV_WIDTH), 0.1),
        "v_lora_up": nrm((Lv, MV_LORA, RWKV_WIDTH), MV_LORA ** -0.5),
        "w_out": nrm((L, D_MODEL, D_MODEL), D_MODEL ** -0.5),
        "ln2_g": 1.0 + nrm((L, D_MODEL), 0.05),
        "w_gate_up": nrm((L, D_MODEL, 2 * D_FF), D_MODEL ** -0.5),
        "w_down": nrm((L, D_FF, D_MODEL), D_FF ** -0.5),
        "final_g": 1.0 + nrm((D_MODEL,), 0.05),
    }


def reference(x, meta_tokens, ln1_g, w_in, mu_shift, pool_w, pool_scale, w0, w_lora_up,
              a0, a_lora_up, g_lora_up, k_k, k_a, r_k, gn_w, gn_b, w_in_vres, mu_vres,
              v0, v_lora_up, w_out, ln2_g, w_gate_up, w_down, final_g):
    f32 = jnp.float32
    bsz = x.shape[0]
    dt = x.dtype
    meta = jnp.broadcast_to(meta_tokens.astype(dt)[None], (bsz, N_META, D_MODEL))
    h = jnp.concatenate([meta, x], axis=1)
    T = h.shape[1]
    v_first = None
    for l in range(DEPTH):
        hn = rms_norm(h, ln1_g[l])
        if l == 0:
            w_comb, mu = w_in[0], mu_shift[0]
            widths = SHIFT_WIDTHS
        else:
            w_comb = jnp.concatenate([w_in[l], w_in_vres[l - 1]], axis=1)
            mu = jnp.concatenate([mu_shift[l], mu_vres[l - 1]])
            widths = SHIFT_WIDTHS + (MV_LORA,)
        proj = hn @ w_comb

        pool_out = multiscale_pool(proj[..., :POOL_WIDTH], pool_w[l], pool_scale[l])

        s = token_shift(proj[..., POOL_WIDTH:].astype(f32), mu.astype(f32))
        parts = split_cols(s, widths)
        r, k, v, w_lo, a_lo, g_lo = parts[:6]
        w_log = -jax.nn.softplus(-(w0[l].astype(f32) + jnp.tanh(w_lo) @ w_lora_up[l].astype(f32))) - 0.5
        decay = jnp.exp(-jnp.exp(w_log))
        a = jax.nn.sigmoid(a0[l].astype(f32) + a_lo @ a_lora_up[l].astype(f32))
        g = jax.nn.sigmoid(g_lo) @ g_lora_up[l].astype(f32)
        if l == 0:
            v_first = v
        else:
            v_mix = jax.nn.sigmoid(v0[l - 1].astype(f32) + parts[6] @ v_lora_up[l - 1].astype(f32))
            v = v + (v_first - v) * v_mix
        kk = to_heads(k * k_k[l].astype(f32))
        kk = kk / jnp.maximum(jnp.linalg.norm(kk, axis=-1, keepdims=True), L2_EPS)
        k = k * (1.0 + (a - 1.0) * k_a[l].astype(f32))
        rh, kh, vh, ah = to_heads(r), to_heads(k), to_heads(v), to_heads(a)
        y = rwkv7_scan(rh, to_heads(decay), kh, vh, -kk, kk * ah)
        mean = jnp.mean(y, axis=-1, keepdims=True)
        var = jnp.mean(jnp.square(y - mean), axis=-1, keepdims=True)
        y = ((y - mean) * lax.rsqrt(var + GN_EPS)).reshape(bsz, T, RWKV_WIDTH)
        y = y * gn_w[l].astype(f32) + gn_b[l].astype(f32)
        bonus = jnp.sum(rh * kh * to_heads(r_k[l].astype(f32)), axis=-1, keepdims=True) * vh
        rwkv_out = ((y + bonus.reshape(bsz, T, RWKV_WIDTH)) * g).astype(dt)

        mixed = jnp.concatenate([pool_out, rwkv_out], axis=-1) @ w_out[l]
        h = h + mixed.astype(dt)

        hn2 = rms_norm(h, ln2_g[l])
        gate, up = jnp.split(hn2 @ w_gate_up[l], 2, axis=-1)
        h = h + ((jax.nn.silu(gate) * up) @ w_down[l]).astype(dt)

    out = rms_norm(h, final_g)
    return out[:, N_META:]
```

```python
import math
import numpy as np
import ml_dtypes
import concourse.bass as bass
import concourse.mybir as mybir
from concourse.bass_utils import run_bass_kernel_spmd

F32 = mybir.dt.float32
BF16 = mybir.dt.bfloat16
AF = mybir.ActivationFunctionType
ALU = mybir.AluOpType
AX = mybir.AxisListType

D = 1024
KC = 8
N_META = 16
POOLW = 512
RW = 512
N_IN = 2336
NINP = 2432
D_FF = 2816
NF = 22
RMS_EPS = 1e-6
GN_EPS = 64e-5
C0 = math.exp(-0.5)
NCOL = 72
ENGS = ("pe", "act", "dve", "pool", "sp")
DBG = {}


class View:
    def __init__(self, ap, res):
        self.ap = ap
        self.res = res


class Prog:
    def __init__(self, nc):
        self.nc = nc
        self.streams = {e: [] for e in ENGS}
        self.cnt = {e: 0 for e in ENGS}
        self.known = {e: {} for e in ENGS}
        self.keys = {}
        self.segs = [[0, 1 << 30, None, []]]
        self.dma_cnt = {}
        self.final_events = []

    def _states(self, res):
        if res[0] == "key":
            st = self.keys.get(res[1])
            if st is None:
                st = [None, []]
                self.keys[res[1]] = st
            return [st]
        _, s, e = res
        out = []
        segs = self.segs
        i = 0
        while i < len(segs):
            sg = segs[i]
            if sg[1] <= s or sg[0] >= e:
                i += 1
                continue
            if sg[0] < s:
                segs.insert(i, [sg[0], s, sg[2], list(sg[3])])
                sg[0] = s
                i += 1
                continue
            if sg[1] > e:
                segs.insert(i + 1, [e, sg[1], sg[2], list(sg[3])])
                sg[1] = e
            out.append(sg)
            i += 1
        return [_SegState(sg) for sg in out]

    def emit(self, eng, f, R=(), W=(), dma=None):
        deps = []
        for v in R:
            for st in self._states(v.res):
                w = st[0]
                if w is not None:
                    deps.append((w, True))
        for v in W:
            for st in self._states(v.res):
                w = st[0]
                if w is not None:
                    deps.append((w, False))
                for r in st[1]:
                    deps.append((r, False))
        waits = []
        kn = self.known[eng]
        need = {}
        for (ev, raw) in deps:
            k, val = ev
            if k == eng and eng == "pe":
                continue
            need[k] = max(need.get(k, 0), val)
        for k, val in need.items():
            if kn.get(k, 0) < val:
                kn[k] = val
                waits.append((k, val))
        if dma is None:
            self.cnt[eng] += 1
            ev = (eng, self.cnt[eng])
        else:
            self.dma_cnt[dma] = self.dma_cnt.get(dma, 0) + 16
            ev = ("dma:" + dma, self.dma_cnt[dma])
        rec = _Rec()
        f(rec)
        self.streams[eng].append((waits, rec.call, ev))
        for v in R:
            for st in self._states(v.res):
                st[1][:] = [r for r in st[1] if r[0] != ev[0]] + [ev]
        for v in W:
            for st in self._states(v.res):
                st[0] = ev
                st[1][:] = []
        return ev


class _Rec:
    def __getattr__(self, name):
        def m(*a, **k):
            self.call = (name, a, k)
            return self
        return m


class _SegState:
    def __init__(self, sg):
        self.sg = sg

    def __getitem__(self, i):
        return self.sg[2 + i]

    def __setitem__(self, i, v):
        self.sg[2 + i] = v


class Scratch:
    def __init__(self, ap_f32, nbytes):
        self.ap = ap_f32
        self.nbytes = nbytes

    def view(self, off, shape, dt):
        sz = 2 if dt == BF16 else 4
        n = int(np.prod(shape))
        nb = n * sz
        assert off % 4 == 0 and off + nb <= self.nbytes, (off, nb, self.nbytes)
        nb4 = (nb + 3) // 4
        a = self.ap[:, off // 4: off // 4 + nb4]
        if dt == BF16:
            a = a.bitcast(BF16)[:, 0:n]
        if len(shape) == 2:
            a = a.rearrange("p (a b) -> p a b", a=shape[0])
        elif len(shape) == 3:
            a = a.rearrange("p (a b c) -> p a b c", a=shape[0], b=shape[1])
        v = View(a, ("reg", off, off + nb4 * 4))
        v.off, v.shape, v.sz = off, tuple(shape), sz
        return v


def vs(v, i, c0=None, c1=None):
    A, Bn = v.shape
    c0 = 0 if c0 is None else c0
    c1 = Bn if c1 is None else c1
    lo = v.off + (i * Bn + c0) * v.sz
    hi = v.off + (i * Bn + c1) * v.sz
    return View(v.ap[:, i, c0:c1], ("reg", lo // 4 * 4, (hi + 3) // 4 * 4))


def vblk(v, i0, i1):
    A, Bn = v.shape
    lo = v.off + i0 * Bn * v.sz
    hi = v.off + i1 * Bn * v.sz
    return View(v.ap[:, i0:i1, :], ("reg", lo, hi))


class Bump:
    def __init__(self, base, limit):
        self.o = base
        self.limit = limit
        self.peak = base

    def take(self, shape, dt):
        sz = 2 if dt == BF16 else 4
        nb = (int(np.prod(shape)) * sz + 31) // 32 * 32
        o = self.o
        self.o += nb
        self.peak = max(self.peak, self.o)
        assert self.o <= self.limit, ("scratch overflow", self.o, self.limit)
        return o


def _ffn_tiling(T):
    nmac = -(-T // 1032)
    macs = []
    base = 0
    for m in range(nmac):
        ml = (T - base) // (nmac - m)
        nsub = -(-ml // 344)
        subs = []
        b2 = base
        for s in range(nsub):
            sl = (base + ml - b2) // (nsub - s)
            subs.append((b2, sl))
            b2 += sl
        macs.append(subs)
        base += ml
    return macs


def build_program(DEPTH, SEQ, NSEQ):
    T = N_META + SEQ
    NT = -(-T // 128)
    TP = NT * 128
    nc = bass.Bass("TRN2", target_bir_lowering=False)
    P = Prog(nc)

    def dram(name, shape, dt=F32, kind="ExternalInput"):
        return nc.dram_tensor(name, list(shape), dt, kind=kind).ap()

    LV = max(DEPTH - 1, 1)
    x_d = dram("x", [NSEQ, SEQ, D])
    meta_d = dram("meta_tokens", [N_META, D])
    w_in_d = dram("w_in", [DEPTH, D, N_IN])
    pool_w_d = dram("pool_w", [DEPTH, 4, 128, 128])
    wup_d = dram("w_lora_up", [DEPTH, 64, RW])
    aup_d = dram("a_lora_up", [DEPTH, 64, RW])
    gup_d = dram("g_lora_up", [DEPTH, 160, RW])
    vres_d = dram("w_in_vres", [LV, D, 32])
    vup_d = dram("v_lora_up", [LV, 32, RW])
    w_out_d = dram("w_out", [DEPTH, D, D])
    wgu_d = dram("w_gate_up", [DEPTH, D, 2 * D_FF])
    wdn_d = dram("w_down", [DEPTH, D_FF, D])
    pv_d = dram("pvec", [128, DEPTH * NCOL + 8])
    c_idf = dram("c_ident_f", [128, 128])
    c_onesf = dram("c_ones_f", [128, 128])
    c_rc0 = dram("c_rc0", [128, 4 * 16])
    c_idb = dram("c_ident_b", [128, 128], BF16)
    c_msk = dram("c_mask", [128, 256], BF16)
    c_blk = dram("c_blk", [128, 128], BF16)
    c_o1k = dram("c_o1k", [128, 128], BF16)
    out_d = dram("out", [NSEQ, SEQ, D], kind="ExternalOutput")

    import contextlib
    es = contextlib.ExitStack()

    def sb(name, shape, dt):
        return es.enter_context(nc.sbuf_tensor(name, list(shape), dt))

    def kv(name, ap):
        return View(ap, ("key", name))

    hT_t = sb("hT", [128, KC, TP], F32)
    vf_t = sb("vf", [128, 4, TP], BF16)
    pv = kv("pv", sb("pv", [128, DEPTH * NCOL + 8], F32)[:])
    idf = kv("idf", sb("idf", [128, 128], F32)[:])
    onesf = kv("onesf", sb("onesf", [128, 128], F32)[:])
    rc0 = kv("rc0", sb("rc0", [128, 4, 16], F32)[:])
    idb = kv("idb", sb("idb", [128, 128], BF16)[:])
    msk = kv("msk", sb("msk", [128, 256], BF16)[:])
    blk = kv("blk", sb("blk", [128, 128], BF16)[:])
    o1k = kv("o1k", sb("o1k", [128, 128], BF16)[:])
    wup = kv("wup", sb("wup", [128, RW], BF16)[:])
    aup = kv("aup", sb("aup", [128, RW], BF16)[:])
    gup0 = kv("gup0", sb("gup0", [128, RW], BF16)[:])
    gup1 = kv("gup1", sb("gup1", [128, RW], BF16)[:])
    vup = kv("vup", sb("vup", [128, RW], BF16)[:])
    plw = kv("plw", sb("plw", [128, 4, 128], BF16)[:])
    prevcol = kv("prevcol", sb("prevcol", [128, 16], F32)[:])
    Ubuf = kv("Ubuf", sb("Ubuf", [128, 4, 144], F32)[:])
    ST = kv("ST", sb("ST", [128, 4, 64], F32)[:])
    STb = kv("STb", sb("STb", [128, 4, 64], BF16)[:])
    rW = kv("rW", sb("rW", [128, 4, 129], F32)[:])
    LA = kv("LA", sb("LA", [128, 128], BF16)[:])
    LG0 = kv("LG0", sb("LG0", [128, 128], BF16)[:])
    LG1 = kv("LG1", sb("LG1", [128, 128], BF16)[:])
    SCR_BYTES = 106 * 1024
    scr_t = sb("scr", [128, SCR_BYTES // 4], F32)
    S = Scratch(scr_t[:], SCR_BYTES)
    psum_t = es.enter_context(nc.psum_tensor("ps", [128, 8, 512], F32))
    banks = [kv("bank%d" % i, psum_t[:, i, :]) for i in range(8)]
    bank_i = [0]

    def nbank():
        b = banks[bank_i[0] % 8]
        bank_i[0] += 1
        return b

    def hT(j):
        return kv("hT%d" % j, hT_t[:, :, j * 128:(j + 1) * 128])

    def hT_range(t0, n):
        return [kv("hT%d" % j, None) for j in range(t0 // 128, (t0 + n - 1) // 128 + 1)]

    def vf(j):
        return kv("vf%d" % j, vf_t[:, :, j * 128:(j + 1) * 128])

    def PE(f, R, W):
        return P.emit("pe", f, R, W)

    def ACT(f, R, W):
        return P.emit("act", f, R, W)

    def DVE(f, R, W):
        return P.emit("dve", f, R, W)

    def POOL(f, R, W):
        return P.emit("pool", f, R, W)

    def DMA(q, name, out_ap, in_ap, R, W):
        return P.emit(q, lambda e: e.dma_start(out=out_ap, in_=in_ap), R, W, dma=name)

    def mm(out_ap, lhsT, rhs, start, stop, R, W):
        PE(lambda e: e.matmul(out_ap, lhsT, rhs, start=start, stop=stop), R, W)

    def tr(out_ap, in_ap, ident, R, W):
        PE(lambda e: e.transpose(out_ap, in_ap, ident.ap), R + [ident], W)

    def bc(ap2, n):
        k = ap2.shape[1]
        return ap2.unsqueeze(2).to_broadcast([128, k, n])

    DMA("sp", "c0", pv.ap, pv_d, [], [pv])
    DMA("sp", "c1", idf.ap, c_idf, [], [idf])
    DMA("sp", "c2", onesf.ap, c_onesf, [], [onesf])
    DMA("sp", "c3", rc0.ap, c_rc0.rearrange("p (a b) -> p a b", a=4), [], [rc0])
    DMA("sp", "c4", idb.ap, c_idb, [], [idb])
    DMA("sp", "c5", msk.ap, c_msk, [], [msk])
    DMA("sp", "c6", blk.ap, c_blk, [], [blk])
    DMA("sp", "c7", o1k.ap, c_o1k, [], [o1k])
    for v in (wup, aup, gup0, gup1, vup, LA, LG0, LG1):
        POOL(lambda e, v=v: e.memset(v.ap, 0.0), [], [v])
    POOL(lambda e: e.memset(rW.ap, 1.0), [], [rW])

    def pcol(l, c0, n=1):
        return pv.ap[:, l * NCOL + c0: l * NCOL + c0 + n]

    FG = DEPTH * NCOL

    def rmsnorm(hviews, cols, n, gcol, sq, rstd, outv, out_slices):
        ACT(lambda e: e.activation(out=sq.ap[:, :, 0:n], in_=hT_t[:, :, cols], func=AF.Square), hviews, [sq])
        bk = nbank()
        for kc in range(KC):
            mm(bk.ap[:, 0:n], o1k.ap, sq.ap[:, kc, 0:n], kc == 0, kc == KC - 1, [sq, o1k], [bk])
        ACT(lambda e: e.activation(out=rstd.ap[:, 0:n], in_=bk.ap[:, 0:n], func=AF.Sqrt, bias=RMS_EPS, scale=1.0),
            [bk], [rstd])
        DVE(lambda e: e.reciprocal(out=rstd.ap[:, 0:n], in_=rstd.ap[:, 0:n]), [rstd], [rstd])
        for kc in range(KC):
            ov = out_slices(kc)
            DVE(lambda e, kc=kc, ov=ov: e.scalar_tensor_tensor(out=ov.ap, in0=hT_t[:, kc, cols],
                                                               scalar=pv.ap[:, gcol + kc: gcol + kc + 1],
                                                               in1=rstd.ap[:, 0:n], op0=ALU.mult, op1=ALU.mult),
                hviews + [rstd, pv], [ov])

    for sq_i in range(NSEQ):
        xs = [S.view(0, [1024], F32), S.view(4096, [1024], F32)]
        for j in range(NT):
            st = xs[j % 2]
            p0 = j * 128
            lo, hi = max(p0, N_META), min(p0 + 128, T)
            if j == 0 or hi < p0 + 128:
                POOL(lambda e, st=st: e.memset(st.ap, 0.0), [], [st])
            nm = "xs%d" % (j % 2)
            if j == 0:
                DMA("sp", nm, st.ap[0:N_META, :], meta_d, [], [st])
            if hi > lo:
                DMA("sp", nm, st.ap[lo - p0:hi - p0, :], x_d[sq_i, lo - N_META:hi - N_META, :], [], [st])
            hv = hT(j)
            for half in range(2):
                bk = nbank()
                for q in range(4):
                    kc = half * 4 + q
                    tr(bk.ap[:, q * 128:(q + 1) * 128], st.ap[:, kc * 128:(kc + 1) * 128], idf, [st], [bk])
                ACT(lambda e, bk=bk, half=half, j=j: e.activation(
                    out=hT_t[:, half * 4:half * 4 + 4, j * 128:(j + 1) * 128],
                    in_=bk.ap.rearrange("p (a b) -> p a b", a=4), func=AF.Copy), [bk], [hv])

        for l in range(DEPTH if not DBG.get('nolayers') else 0):
            B = Bump(0, SCR_BYTES)

            def V(shape, dt, off=None):
                return S.view(B.take(shape, dt) if off is None else off, shape, dt)

            gcols = [(0, 512), (512, 1024), (1024, 1536), (1536, 2048), (2048, NINP)]
            w_in_g = [V([KC, a1_ - a0_], BF16) for (a0_, a1_) in gcols]
            wout = V([KC, D], BF16)
            POOL(lambda e: e.memset(w_in_g[4].ap[:, :, N_IN - 2048:NINP - 2048], 0.0), [], [w_in_g[4]])
            for gi, (a0_, a1_) in enumerate(gcols):
                a1r = min(a1_, N_IN)
                DMA("pool", "win%d" % gi, w_in_g[gi].ap[:, :, 0:a1r - a0_],
                    w_in_d[l, :, a0_:a1r].rearrange("(k p) n -> p k n", p=128), [], [w_in_g[gi]])
            if l >= 1:
                DMA("pool", "win4", w_in_g[4].ap[:, :, N_IN - 2048:N_IN - 2048 + 32],
                    vres_d[l - 1].rearrange("(k p) n -> p k n", p=128), [], [w_in_g[4]])
            DMA("pool", "wout", wout.ap, w_out_d[l].rearrange("(k p) n -> p k n", p=128), [], [wout])
            DMA("pool", "wup", wup.ap[0:64, :], wup_d[l], [], [wup])
            DMA("pool", "aup", aup.ap[64:128, :], aup_d[l], [], [aup])
            DMA("pool", "gup0", gup0.ap, gup_d[l, 0:128, :], [], [gup0])
            DMA("pool", "gup1", gup1.ap[0:32, :], gup_d[l, 128:160, :], [], [gup1])
            if l >= 1:
                DMA("pool", "vup", vup.ap[32:64, :], vup_d[l - 1], [], [vup])
            DMA("pool", "plw", plw.ap, pool_w_d[l].rearrange("g c d -> c g d"), [], [plw])

            if DBG.get('gate'):
                gate = kv("gate", prevcol.ap[:, 15:16])
                POOL(lambda e: e.memset(gate.ap, 0.0), list(w_in_g) + [wout, wup, aup, gup0, gup1, vup, plw], [gate])
            ARe = V([4, 2, 128], BF16)
            ARo = V([4, 2, 128], BF16)
            BTe, BTo, KTe, KTo, BPe, BPo, KPe, KPo = [V([4, 128], BF16) for _ in range(8)]
            for v in (ARe, ARo, BTe, BTo, KTe, KTo, BPe, BPo, KPe, KPo):
                POOL(lambda e, v=v: e.memset(v.ap, 0.0), [], [v])
            cat = V([8, 128], BF16)
            bonus = V([4, 128], F32)
            gT = V([4, 128], F32)
            Vtok = V([8, 64], BF16)
            bpv = V([4, 128], BF16)
            kpv = V([4, 128], BF16)
            vb = V([4, 128], BF16)
            kksq = V([4, 128], BF16)
            stats = V([8, 8], F32)
            ncl = V([4], F32)
            m0 = B.o
            r_ = V([4, 128], F32)
            k_ = V([4, 128], F32)
            v_ = V([4, 128], F32)
            tmpA = V([4, 128], F32)
            m1 = B.o
            sqv = V([8, 128], BF16)
            hn = V([8, 128], BF16)
            rstd = V([128], F32)
            Pg = [V([4, 129], F32), V([4, 129], F32)]
            La = V([4, 144], F32)
            Lb = V([4, 144], F32)
            diff = V([4, 128], BF16)
            e1 = B.o
            B.o = m1
            sg = V([4, 128], F32)
            cl = V([4, 128], F32)
            iW = V([4, 128], F32)
            dW = V([4, 128], F32)
            a_ = V([4, 128], F32)
            kk = V([4, 128], F32)
            knew = V([4, 128], F32)
            b_ = V([4, 128], F32)
            tmpB = V([4, 128], F32)
            e2 = B.o
            B.o = m0
            Amat = V([8, 4, 128], BF16)
            TTf = V([8, 128], BF16)
            X0 = V([8, 128], BF16)
            Xp = V([8, 128], BF16)
            XTp = V([8, 128], BF16)
            TTp = V([8, 128], BF16)
            X1b = V([8, 64], BF16)
            Ub = V([8, 64], BF16)
            Ytok = V([8, 64], F32)
            Ysq = V([8, 64], F32)
            yn = V([8, 64], BF16)
            yg = V([4, 128], F32)
            B.o = max(B.o, e1, e2)

            g1c = l * NCOL
            POOL(lambda e: e.memset(prevcol.ap, 0.0), [], [prevcol])
            POOL(lambda e: e.memset(Ubuf.ap, 0.0), [], [Ubuf])
            POOL(lambda e: e.memset(ST.ap, 0.0), [], [ST])
            POOL(lambda e: e.memset(STb.ap, 0.0), [], [STb])

            for j in range(NT if not DBG.get('nomixer') else 0):
                cols = slice(j * 128, (j + 1) * 128)
                hv = hT(j)
                if j == 0:
                    rmsnorm([hv], cols, 128, g1c + 0, sqv, rstd, hn, lambda kc: vs(hn, kc))

                if DBG.get('stage', 99) < 1:
                    continue
                def inproj(gi, nchunk):
                    bk = nbank()
                    for ci in range(nchunk):
                        c = ci * 128
                        for kc in range(KC):
                            mm(bk.ap[:, ci * 128:(ci + 1) * 128], w_in_g[gi].ap[:, kc, c:c + 128], hn.ap[:, kc, :],
                               kc == 0, kc == KC - 1, [w_in_g[gi], hn], [bk])
                    return bk

                bk = inproj(0, 4)
                ACT(lambda e, bk=bk: e.activation(out=Ubuf.ap[:, :, 16:144],
                                                  in_=bk.ap.rearrange("p (a b) -> p a b", a=4), func=AF.Copy),
                    [bk], [Ubuf])

                def shift(gi, nchunk, outv):
                    bk = inproj(gi, nchunk)
                    pg = Pg[gi % 2]
                    pc = prevcol.ap[:, (gi - 1) * 4:(gi - 1) * 4 + nchunk]
                    POOL(lambda e: e.tensor_copy(out=pg.ap[:, 0:nchunk, 0:1], in_=pc.unsqueeze(2)), [prevcol], [pg])
                    ACT(lambda e: e.activation(out=pg.ap[:, 0:nchunk, 1:129],
                                               in_=bk.ap[:, 0:nchunk * 128].rearrange("p (a b) -> p a b", a=nchunk),
                                               func=AF.Copy), [bk], [pg])
                    POOL(lambda e: e.tensor_copy(out=pc.unsqueeze(2), in_=pg.ap[:, 0:nchunk, 128:129]), [pg], [prevcol])
                    mub = bc(pcol(l, 16 + (gi - 1) * 4, nchunk), 128)
                    DVE(lambda e: e.tensor_tensor(out=tmpA.ap[:, 0:nchunk, :], in0=pg.ap[:, 0:nchunk, 0:128],
                                                  in1=pg.ap[:, 0:nchunk, 1:129], op=ALU.subtract), [pg], [tmpA])
                    DVE(lambda e: e.tensor_tensor(out=tmpA.ap[:, 0:nchunk, :], in0=tmpA.ap[:, 0:nchunk, :],
                                                  in1=mub, op=ALU.mult), [tmpA, pv], [tmpA])
                    DVE(lambda e: e.tensor_tensor(out=outv.ap[:, 0:nchunk, :], in0=tmpA.ap[:, 0:nchunk, :],
                                                  in1=pg.ap[:, 0:nchunk, 1:129], op=ALU.add), [tmpA, pg], [outv])

                U = Ubuf.ap
                POOL(lambda e: e.tensor_tensor(out=La.ap[:, 0:4, 1:144], in0=U[:, 0:4, 1:144], in1=U[:, 0:4, 0:143],
                                               op=ALU.add), [Ubuf], [La])
                POOL(lambda e: e.tensor_tensor(out=Lb.ap[:, 1:4, 3:144], in0=La.ap[:, 1:4, 3:144],
                                               in1=La.ap[:, 1:4, 1:142], op=ALU.add), [La], [Lb])
                POOL(lambda e: e.tensor_tensor(out=La.ap[:, 2:4, 7:144], in0=Lb.ap[:, 2:4, 7:144],
                                               in1=Lb.ap[:, 2:4, 3:140], op=ALU.add), [Lb], [La])
                POOL(lambda e: e.tensor_tensor(out=Lb.ap[:, 3:4, 15:144], in0=La.ap[:, 3:4, 15:144],
                                               in1=La.ap[:, 3:4, 7:136], op=ALU.add), [La], [Lb])
                shift(4, 3, sg)
                ACT(lambda e: e.activation(out=LA.ap[0:64, :], in_=sg.ap[0:64, 0, :], func=AF.Tanh), [sg], [LA])
                ACT(lambda e: e.activation(out=LA.ap[64:128, :], in_=sg.ap[64:128, 0, :], func=AF.Copy), [sg], [LA])
                ACT(lambda e: e.activation(out=LG0.ap, in_=sg.ap[:, 1, :], func=AF.Sigmoid), [sg], [LG0])
                ACT(lambda e: e.activation(out=LG1.ap[0:32, :], in_=sg.ap[0:32, 2, :], func=AF.Sigmoid), [sg], [LG1])
                ACT(lambda e: e.activation(out=LG1.ap[32:64, :], in_=sg.ap[32:64, 2, :], func=AF.Copy), [sg], [LG1])

                shift(2, 4, k_)
                shift(1, 4, r_)
                shift(3, 4, v_)
                def lora4(lhs, rhsv, extra=None):
                    bk = nbank()
                    for hp in range(4):
                        if extra is None:
                            mm(bk.ap[:, hp * 128:(hp + 1) * 128], lhs.ap[:, hp * 128:(hp + 1) * 128], rhsv.ap,
                               True, True, [lhs, rhsv], [bk])
                        else:
                            mm(bk.ap[:, hp * 128:(hp + 1) * 128], lhs.ap[:, hp * 128:(hp + 1) * 128], rhsv.ap,
                               True, False, [lhs, rhsv], [bk])
                            mm(bk.ap[:, hp * 128:(hp + 1) * 128], extra[0].ap[:, hp * 128:(hp + 1) * 128], extra[1].ap,
                               False, True, [extra[0], extra[1]], [bk])
                    return bk

                bkw = lora4(wup, LA)
                bka = lora4(aup, LA)
                bkg = lora4(gup0, LG0, (gup1, LG1))
                if l >= 1:
                    bkv = lora4(vup, LG1)
                for g in range(4):
                    src = La if g % 2 == 0 else Lb
                    DVE(lambda e, g=g, src=src: e.scalar_tensor_tensor(
                        out=diff.ap[:, g, :], in0=src.ap[:, g, 16:144], scalar=1.0 / (2 << g),
                        in1=U[:, g, 16:144], op0=ALU.mult, op1=ALU.subtract), [src, Ubuf], [diff])
                    if j == 0:
                        DVE(lambda e, g=g, src=src: e.tensor_tensor(out=tmpA.ap[:, g, 0:16], in0=src.ap[:, g, 16:32],
                                                                    in1=rc0.ap[:, g, :], op=ALU.mult),
                            [src, rc0], [tmpA])
                        DVE(lambda e, g=g: e.tensor_tensor(out=diff.ap[:, g, 0:16], in0=tmpA.ap[:, g, 0:16],
                                                           in1=U[:, g, 16:32], op=ALU.subtract), [tmpA, Ubuf], [diff])
                POOL(lambda e: e.tensor_copy(out=U[:, :, 0:16], in_=U[:, :, 128:144]), [Ubuf, La, Lb, diff], [Ubuf])
                bk = nbank()
                for g in range(4):
                    mm(bk.ap[:, g * 128:(g + 1) * 128], plw.ap[:, g, :], diff.ap[:, g, :], True, True, [plw, diff], [bk])
                DVE(lambda e, bk=bk: e.tensor_tensor(out=cat.ap[:, 0:4, :], in0=bk.ap.rearrange("p (a b) -> p a b", a=4),
                                                     in1=bc(pcol(l, 31, 4), 128), op=ALU.mult), [bk, pv], [cat])

                for hp in range(4):
                    ACT(lambda e, hp=hp: e.activation(out=sg.ap[:, hp, :], in_=bkw.ap[:, hp * 128:(hp + 1) * 128],
                                                      func=AF.Sigmoid, bias=pcol(l, 35 + hp)), [bkw, pv], [vs(sg, hp)])
                for hp in range(4):
                    ACT(lambda e, hp=hp: e.activation(out=a_.ap[:, hp, :], in_=bka.ap[:, hp * 128:(hp + 1) * 128],
                                                      func=AF.Sigmoid, bias=pcol(l, 39 + hp)), [bka, pv], [vs(a_, hp)])
                ACT(lambda e: e.activation(out=gT.ap, in_=bkg.ap.rearrange("p (a b) -> p a b", a=4), func=AF.Copy),
                    [bkg], [gT])
                if l >= 1:
                    for hp in range(4):
                        ACT(lambda e, hp=hp: e.activation(out=tmpB.ap[:, hp, :], in_=bkv.ap[:, hp * 128:(hp + 1) * 128],
                                                          func=AF.Sigmoid, bias=pcol(l, 55 + hp)), [bkv, pv], [vs(tmpB, hp)])
                    vfj = vf(j)
                    DVE(lambda e: e.tensor_tensor(out=tmpA.ap, in0=vfj.ap, in1=v_.ap, op=ALU.subtract), [vfj, v_], [tmpA])
                    DVE(lambda e: e.tensor_tensor(out=tmpA.ap, in0=tmpA.ap, in1=tmpB.ap, op=ALU.mult), [tmpA, tmpB], [tmpA])
                    DVE(lambda e: e.tensor_tensor(out=v_.ap, in0=v_.ap, in1=tmpA.ap, op=ALU.add), [v_, tmpA], [v_])
                else:
                    vfj = vf(j)
                    POOL(lambda e: e.tensor_copy(out=vfj.ap, in_=v_.ap), [v_], [vfj])
                for hp in range(4):
                    DVE(lambda e, hp=hp: e.tensor_tensor_scan(out=cl.ap[:, hp, :], data0=onesf.ap, data1=sg.ap[:, hp, :],
                                                              initial=0.0, op0=ALU.mult, op1=ALU.add), [vs(sg, hp), onesf], [vs(cl, hp)])
                ACT(lambda e: e.activation(out=rW.ap[:, :, 1:129], in_=cl.ap, func=AF.Exp, scale=-C0), [cl], [rW])
                ACT(lambda e: e.activation(out=iW.ap, in_=cl.ap, func=AF.Exp, scale=C0), [cl], [iW])
                DVE(lambda e: e.tensor_scalar(out=ncl.ap.unsqueeze(2), in0=cl.ap[:, :, 127:128], scalar1=-C0, scalar2=None,
                                              op0=ALU.mult), [cl], [ncl])
                for hp in range(4):
                    ACT(lambda e, hp=hp: e.activation(out=dW.ap[:, hp, :], in_=cl.ap[:, hp, :], func=AF.Exp, scale=C0,
                                                      bias=ncl.ap[:, hp:hp + 1]), [vs(cl, hp), ncl], [vs(dW, hp)])
                DVE(lambda e: e.tensor_tensor(out=kk.ap, in0=k_.ap, in1=bc(pcol(l, 43, 4), 128), op=ALU.mult),
                    [k_, pv], [kk])
                POOL(lambda e: e.tensor_tensor(out=kksq.ap, in0=kk.ap, in1=kk.ap, op=ALU.mult), [kk], [kksq])
                bk = nbank()
                for hp in range(4):
                    mm(bk.ap[:, hp * 128:(hp + 1) * 128], blk.ap, kksq.ap[:, hp, :], True, True, [blk, kksq], [bk])
                ACT(lambda e, bk=bk: e.activation(out=tmpA.ap, in_=bk.ap.rearrange("p (a b) -> p a b", a=4),
                                                  func=AF.Sqrt, bias=1e-24, scale=1.0), [bk], [tmpA])
                DVE(lambda e: e.reciprocal(out=tmpA.ap, in_=tmpA.ap), [tmpA], [tmpA])
                DVE(lambda e: e.tensor_tensor(out=kk.ap, in0=kk.ap, in1=tmpA.ap, op=ALU.mult), [kk, tmpA], [kk])
                DVE(lambda e: e.scalar_tensor_tensor(out=tmpA.ap, in0=a_.ap, scalar=-1.0, in1=bc(pcol(l, 47, 4), 128),
                                                     op0=ALU.add, op1=ALU.mult), [a_, pv], [tmpA])
                DVE(lambda e: e.scalar_tensor_tensor(out=knew.ap, in0=tmpA.ap, scalar=1.0, in1=k_.ap,
                                                     op0=ALU.add, op1=ALU.mult), [tmpA, k_], [knew])
                POOL(lambda e: e.tensor_tensor(out=b_.ap, in0=kk.ap, in1=a_.ap, op=ALU.mult), [kk, a_], [b_])
                POOL(lambda e: e.tensor_tensor(out=tmpB.ap, in0=knew.ap, in1=bc(pcol(l, 51, 4), 128), op=ALU.mult),
                     [knew, pv], [tmpB])
                POOL(lambda e: e.tensor_tensor(out=kksq.ap, in0=tmpB.ap, in1=r_.ap, op=ALU.mult), [tmpB, r_], [kksq])
                bk = nbank()
                for hp in range(4):
                    mm(bk.ap[:, hp * 128:(hp + 1) * 128], blk.ap, kksq.ap[:, hp, :], True, True, [blk, kksq], [bk])
                DVE(lambda e, bk=bk: e.tensor_tensor(out=bonus.ap, in0=bk.ap.rearrange("p (a b) -> p a b", a=4),
                                                     in1=v_.ap, op=ALU.mult), [bk, v_], [bonus])
                for e_, (AR, BT, KT) in enumerate(((ARe, BTe, KTe), (ARo, BTo, KTo))):
                    ps_ = slice(64 * e_, 64 * e_ + 64)
                    DVE(lambda e, AR=AR, ps_=ps_: e.scalar_tensor_tensor(
                        out=AR.ap[ps_, :, 0, :], in0=kk.ap[ps_], scalar=-1.0, in1=rW.ap[ps_, :, 0:128],
                        op0=ALU.mult, op1=ALU.mult), [kk, rW], [AR])
                    POOL(lambda e, AR=AR, ps_=ps_: e.tensor_tensor(out=AR.ap[ps_, :, 1, :], in0=r_.ap[ps_],
                                                                   in1=rW.ap[ps_, :, 1:129], op=ALU.mult), [r_, rW], [AR])
                    DVE(lambda e, BT=BT, ps_=ps_: e.tensor_tensor(out=BT.ap[ps_], in0=b_.ap[ps_], in1=iW.ap[ps_],
                                                                  op=ALU.mult), [b_, iW], [BT])
                    POOL(lambda e, KT=KT, ps_=ps_: e.tensor_tensor(out=KT.ap[ps_], in0=knew.ap[ps_], in1=iW.ap[ps_],
                                                                   op=ALU.mult), [knew, iW], [KT])
                DVE(lambda e: e.tensor_tensor(out=bpv.ap, in0=b_.ap, in1=dW.ap, op=ALU.mult), [b_, dW], [bpv])
                POOL(lambda e: e.tensor_tensor(out=kpv.ap, in0=knew.ap, in1=dW.ap, op=ALU.mult), [knew, dW], [kpv])
                ACT(lambda e: e.activation(out=vb.ap, in_=v_.ap, func=AF.Copy), [v_], [vb])

                if DBG.get('stage', 99) < 4:
                    continue
                def tr4(src):
                    bk = nbank()
                    for hp in range(4):
                        mm(bk.ap[:, hp * 128:(hp + 1) * 128], src.ap[:, hp, :], idb.ap, True, True, [src, idb], [bk])
                    return bk, bk.ap.rearrange("p (a b) -> p a b", a=4)

                bk, bb = tr4(bpv)
                if DBG.get('sub', 9) < 2:
                    continue
                ACT(lambda e, bb=bb: e.activation(out=BPe.ap[:, :, 0:64], in_=bb[:, :, 0:64], func=AF.Copy), [bk], [BPe])
                if DBG.get('sub', 9) < 3:
                    continue
                ACT(lambda e, bb=bb: e.activation(out=BPo.ap[:, :, 64:128], in_=bb[:, :, 64:128], func=AF.Copy), [bk], [BPo])
                if DBG.get('sub', 9) < 4:
                    continue
                bk, bb = tr4(kpv)
                ACT(lambda e, bb=bb: e.activation(out=KPe.ap[:, :, 0:64], in_=bb[:, :, 0:64], func=AF.Copy), [bk], [KPe])
                ACT(lambda e, bb=bb: e.activation(out=KPo.ap[:, :, 64:128], in_=bb[:, :, 64:128], func=AF.Copy), [bk], [KPo])
                bk, bb = tr4(vb)
                ACT(lambda e, bb=bb: e.activation(out=Vtok.ap.rearrange("p a b -> p (a b)"),
                                                  in_=bb.rearrange("p a b -> p (a b)"), func=AF.Copy), [bk], [Vtok])

                if DBG.get('stage', 99) < 5:
                    continue
                for h in range(8):
                    hp, e_ = h // 2, h % 2
                    AR, BT, KT = (ARe, BTe, KTe) if e_ == 0 else (ARo, BTo, KTo)
                    bk = nbank()
                    rhs = AR.ap[:, hp, :, :].rearrange("p a b -> p (a b)")
                    mm(bk.ap[:, 0:256], BT.ap[:, hp, :], rhs, True, True, [BT, AR], [bk])
                    mm(bk.ap[:, 256:512], KT.ap[:, hp, :], rhs, True, True, [KT, AR], [bk])
                    DVE(lambda e, bk=bk, h=h: e.tensor_tensor(
                        out=Amat.ap[:, h, :, :].rearrange("p (a b) c -> p a b c", a=2),
                        in0=bk.ap.rearrange("p (a b c) -> p a b c", a=2, b=2),
                        in1=msk.ap.rearrange("p (b c) -> p b c", b=2).unsqueeze(1).to_broadcast([128, 2, 2, 128]),
                        op=ALU.mult), [bk, msk], [View(None, ("reg", Amat.off + h * 1024, Amat.off + (h + 1) * 1024))])
                for half in range(2):
                    bk = nbank()
                    for q in range(4):
                        mm(bk.ap[:, q * 128:(q + 1) * 128], Amat.ap[:, half * 4 + q, 0, :], idb.ap, True, True,
                           [Amat, idb], [bk])
                    ACT(lambda e, bk=bk, half=half: e.activation(
                        out=X0.ap[:, half * 4:half * 4 + 4, :].rearrange("p a b -> p (a b)"), in_=bk.ap, func=AF.Copy),
                        [bk], [X0])
                if DBG.get('stage', 99) < 6:
                    continue
                def blkv(v, hb):
                    return vblk(v, hb * 4, hb * 4 + 4)

                XTping = [View(Amat.ap[:, hb * 4:hb * 4 + 4, 0, :], Amat.res) for hb in range(2)]
                ping = ([blkv(X0, 0), blkv(X0, 1)], XTping, [blkv(TTf, 0), blkv(TTf, 1)])
                pong = ([blkv(Xp, 0), blkv(Xp, 1)], [blkv(XTp, 0), blkv(XTp, 1)], [blkv(TTp, 0), blkv(TTp, 1)])
                for hb in range(2):
                    DVE(lambda e, hb=hb: e.tensor_tensor(
                        out=ping[2][hb].ap, in0=ping[1][hb].ap, in1=idb.ap.unsqueeze(1).to_broadcast([128, 4, 128]),
                        op=ALU.add), [ping[1][hb], idb], [ping[2][hb]])
                for lvl in range(6):
                    (Xc, XTc, TTc), (Xn, XTn, TTn) = (ping, pong) if lvl % 2 == 0 else (pong, ping)
                    bkx = [nbank(), nbank()]
                    for hb in range(2):
                        for i in range(4):
                            mm(bkx[hb].ap[:, i * 128:(i + 1) * 128], XTc[hb].ap[:, i, :], Xc[hb].ap[:, i, :], True, True,
                               [XTc[hb], Xc[hb]], [bkx[hb]])
                    if lvl < 5:
                        bkt = [nbank(), nbank()]
                        for hb in range(2):
                            for i in range(4):
                                mm(bkt[hb].ap[:, i * 128:(i + 1) * 128], Xc[hb].ap[:, i, :], XTc[hb].ap[:, i, :], True, True,
                                   [XTc[hb], Xc[hb]], [bkt[hb]])
                    for hb in range(2):
                        ACT(lambda e, hb=hb: e.activation(out=Xn[hb].ap, in_=bkx[hb].ap.rearrange("p (a b) -> p a b", a=4),
                                                          func=AF.Copy), [bkx[hb]], [Xn[hb]])
                    if lvl < 5:
                        ACT(lambda e: e.activation(out=XTn[0].ap, in_=bkt[0].ap.rearrange("p (a b) -> p a b", a=4),
                                                   func=AF.Copy), [bkt[0]], [XTn[0]])
                        DVE(lambda e: e.tensor_scalar(out=XTn[1].ap, in0=bkt[1].ap.rearrange("p (a b) -> p a b", a=4),
                                                      scalar1=1.0, scalar2=None, op0=ALU.mult), [bkt[1]], [XTn[1]])
                    bkT = [nbank(), nbank()]
                    for hb in range(2):
                        for i in range(4):
                            o = bkT[hb].ap[:, i * 128:(i + 1) * 128]
                            mm(o, Xn[hb].ap[:, i, :], TTc[hb].ap[:, i, :], True, True, [Xn[hb], TTc[hb]], [bkT[hb]])
                    for hb in range(2):
                        DVE(lambda e, hb=hb: e.tensor_tensor(out=TTn[hb].ap,
                                                             in0=bkT[hb].ap.rearrange("p (a b) -> p a b", a=4),
                                                             in1=TTc[hb].ap, op=ALU.add), [bkT[hb], TTc[hb]], [TTn[hb]])

                if DBG.get('stage', 99) < 7:
                    continue
                def pad(e_):
                    return (ARe, BPe, KPe) if e_ == 0 else (ARo, BPo, KPo)

                bk1 = nbank()
                for h in range(8):
                    hp, e_ = h // 2, h % 2
                    AR = pad(e_)[0]
                    o = bk1.ap[:, h * 64:(h + 1) * 64]
                    mm(o, AR.ap[:, hp, 0, :], STb.ap[:, hp, :], True, False, [AR, STb], [bk1])
                    mm(o, Amat.ap[:, h, 2, :], Vtok.ap[:, h, :], False, True, [Amat, Vtok], [bk1])
                ACT(lambda e: e.activation(out=X1b.ap.rearrange("p a b -> p (a b)"), in_=bk1.ap, func=AF.Copy), [bk1], [X1b])
                bk2 = nbank()
                for h in range(8):
                    mm(bk2.ap[:, h * 64:(h + 1) * 64], TTf.ap[:, h, :], X1b.ap[:, h, :], True, True, [TTf, X1b], [bk2])
                ACT(lambda e: e.activation(out=Ub.ap.rearrange("p a b -> p (a b)"), in_=bk2.ap, func=AF.Copy), [bk2], [Ub])
                bk3 = nbank()
                for h in range(8):
                    hp, e_ = h // 2, h % 2
                    AR = pad(e_)[0]
                    o = bk3.ap[:, h * 64:(h + 1) * 64]
                    mm(o, AR.ap[:, hp, 1, :], STb.ap[:, hp, :], True, False, [AR, STb], [bk3])
                    mm(o, Amat.ap[:, h, 1, :], Ub.ap[:, h, :], False, False, [Amat, Ub], [bk3])
                    mm(o, Amat.ap[:, h, 3, :], Vtok.ap[:, h, :], False, True, [Amat, Vtok], [bk3])
                bk4 = nbank()
                for hp in range(4):
                    o = bk4.ap[:, hp * 64:(hp + 1) * 64]
                    for e_ in range(2):
                        h = 2 * hp + e_
                        _, BPx, KPx = pad(e_)
                        mm(o, BPx.ap[:, hp, :], Ub.ap[:, h, :], e_ == 0, False, [BPx, Ub], [bk4])
                        mm(o, KPx.ap[:, hp, :], Vtok.ap[:, h, :], False, e_ == 1, [KPx, Vtok], [bk4])
                for hp in range(4):
                    DVE(lambda e, hp=hp: e.scalar_tensor_tensor(out=ST.ap[:, hp, :], in0=ST.ap[:, hp, :],
                                                                scalar=rW.ap[:, hp, 128:129],
                                                                in1=bk4.ap[:, hp * 64:(hp + 1) * 64],
                                                                op0=ALU.mult, op1=ALU.add), [ST, rW, bk4], [ST])
                ACT(lambda e: e.activation(out=STb.ap, in_=ST.ap, func=AF.Copy), [ST], [STb])

                if DBG.get('stage', 99) < 8:
                    continue
                ACT(lambda e: e.activation(out=Ytok.ap.rearrange("p a b -> p (a b)"), in_=bk3.ap, func=AF.Copy), [bk3], [Ytok])
                s1 = stats.ap[:, 0, :]
                s2 = stats.ap[:, 1, :]
                mean = stats.ap[:, 2, :]
                msq = stats.ap[:, 3, :]
                var = stats.ap[:, 4, :]
                rs = stats.ap[:, 5, :]
                DVE(lambda e: e.tensor_reduce(out=s1, in_=Ytok.ap, axis=AX.X, op=ALU.add), [Ytok], [stats])
                POOL(lambda e: e.tensor_tensor(out=Ysq.ap, in0=Ytok.ap, in1=Ytok.ap, op=ALU.mult), [Ytok], [Ysq])
                DVE(lambda e: e.tensor_reduce(out=s2, in_=Ysq.ap, axis=AX.X, op=ALU.add), [Ysq, stats], [stats])
                DVE(lambda e: e.tensor_scalar(out=mean, in0=s1, scalar1=1.0 / 64, scalar2=None, op0=ALU.mult), [stats], [stats])
                DVE(lambda e: e.tensor_tensor(out=msq, in0=mean, in1=mean, op=ALU.mult), [stats], [stats])
                DVE(lambda e: e.scalar_tensor_tensor(out=var, in0=s2, scalar=1.0 / 64, in1=msq, op0=ALU.mult,
                                                     op1=ALU.subtract), [stats], [stats])
                ACT(lambda e: e.activation(out=rs, in_=var, func=AF.Sqrt, bias=GN_EPS, scale=1.0), [stats], [stats])
                DVE(lambda e: e.reciprocal(out=rs, in_=rs), [stats], [stats])
                DVE(lambda e: e.tensor_tensor(out=Ysq.ap, in0=Ytok.ap, in1=mean.unsqueeze(2).to_broadcast([128, 8, 64]),
                                              op=ALU.subtract), [Ytok, stats, Ysq], [Ysq])
                DVE(lambda e: e.tensor_tensor(out=yn.ap, in0=Ysq.ap, in1=rs.unsqueeze(2).to_broadcast([128, 8, 64]),
                                              op=ALU.mult), [Ysq, stats], [yn])
                bk = nbank()
                bb = bk.ap
                ynf = yn.ap.rearrange("p a b -> p (a b)")
                for hp in range(4):
                    mm(bb[:, hp * 128:(hp + 1) * 128], ynf[:, hp * 128:(hp + 1) * 128], idb.ap, True, True, [yn, idb], [bk])
                for hp in range(4):
                    ACT(lambda e, hp=hp, bb=bb: e.activation(out=yg.ap[:, hp, :], in_=bb[:, hp * 128:(hp + 1) * 128],
                                                             func=AF.Identity, scale=pcol(l, 59 + hp),
                                                             bias=pcol(l, 63 + hp)), [bk, pv], [vs(yg, hp)])
                DVE(lambda e: e.tensor_tensor(out=yg.ap, in0=yg.ap, in1=bonus.ap, op=ALU.add), [yg, bonus], [yg])
                DVE(lambda e: e.tensor_tensor(out=cat.ap[:, 4:8, :], in0=yg.ap, in1=gT.ap, op=ALU.mult), [yg, gT], [cat])

                if DBG.get('stage', 99) < 9:
                    continue
                if j + 1 < NT:
                    rmsnorm([hT(j + 1)], slice((j + 1) * 128, (j + 2) * 128), 128, g1c + 0, sqv, rstd, hn,
                            lambda kc: vs(hn, kc))
                for half in range(2):
                    bk = nbank()
                    for q in range(4):
                        oc = half * 4 + q
                        for kc in range(KC):
                            mm(bk.ap[:, q * 128:(q + 1) * 128], wout.ap[:, kc, oc * 128:(oc + 1) * 128], cat.ap[:, kc, :],
                               kc == 0, kc == KC - 1, [wout, cat], [bk])
                    DVE(lambda e, bk=bk, half=half, cols=cols: e.tensor_tensor(
                        out=hT_t[:, half * 4:half * 4 + 4, cols], in0=hT_t[:, half * 4:half * 4 + 4, cols],
                        in1=bk.ap.rearrange("p (a b) -> p a b", a=4), op=ALU.add), [bk, hv], [hv])

            g2c = l * NCOL + 8
            for subs in (_ffn_tiling(T) if not DBG.get('noffn') else []):
                F = Bump(0, SCR_BYTES)
                mt0 = subs[0][0]
                ml = sum(n for _, n in subs)
                act = S.view(F.take([NF, 1032], BF16), [NF, 1032], BF16)
                hn2 = S.view(F.take([KC, 1032], BF16), [KC, 1032], BF16)
                sq2 = S.view(F.take([KC, 128], BF16), [KC, 128], BF16)
                rstd2 = S.view(F.take([128], F32), [128], F32)
                sil = [S.view(F.take([344], F32), [344], F32) for _ in range(2)]
                wg = [S.view(F.take([KC, 256], BF16), [KC, 256], BF16) for _ in range(2)]
                wu = [S.view(F.take([KC, 256], BF16), [KC, 256], BF16) for _ in range(2)]
                wd = [S.view(F.take([NF, 256], BF16), [NF, 256], BF16) for _ in range(2)]
                pc0 = mt0
                while pc0 < mt0 + ml:
                    pn = min((pc0 // 128 + 1) * 128, mt0 + ml) - pc0
                    o = pc0 - mt0
                    rmsnorm(hT_range(pc0, pn), slice(pc0, pc0 + pn), pn, g2c, sq2, rstd2, hn2,
                            lambda kc, o=o, pn=pn: vs(hn2, kc, o, o + pn))
                    pc0 += pn
                for fp in range(NF // 2):
                    sl = fp % 2
                    DMA("pool", "wg%d" % sl, wg[sl].ap,
                        wgu_d[l, :, fp * 256:(fp + 1) * 256].rearrange("(k p) n -> p k n", p=128), [], [wg[sl]])
                    DMA("pool", "wu%d" % sl, wu[sl].ap,
                        wgu_d[l, :, D_FF + fp * 256:D_FF + (fp + 1) * 256].rearrange("(k p) n -> p k n", p=128),
                        [], [wu[sl]])
                    for fl in range(2):
                        f = fp * 2 + fl
                        for si, (s0, n) in enumerate(subs):
                            o = s0 - mt0
                            bg = nbank()
                            bu = nbank()
                            for kc in range(KC):
                                mm(bg.ap[:, 0:n], wg[sl].ap[:, kc, fl * 128:(fl + 1) * 128], hn2.ap[:, kc, o:o + n],
                                   kc == 0, kc == KC - 1, [wg[sl], hn2], [bg])
                            for kc in range(KC):
                                mm(bu.ap[:, 0:n], wu[sl].ap[:, kc, fl * 128:(fl + 1) * 128], hn2.ap[:, kc, o:o + n],
                                   kc == 0, kc == KC - 1, [wu[sl], hn2], [bu])
                            sv = sil[(f * 3 + si) % 2]
                            ACT(lambda e, bg=bg, sv=sv, n=n: e.activation(out=sv.ap[:, 0:n], in_=bg.ap[:, 0:n], func=AF.Silu),
                                [bg], [sv])
                            DVE(lambda e, bu=bu, sv=sv, n=n, f=f, o=o: e.tensor_tensor(
                                out=act.ap[:, f, o:o + n], in0=bu.ap[:, 0:n], in1=sv.ap[:, 0:n], op=ALU.mult),
                                [bu, sv], [vs(act, f, o, o + n)])
                for ocp in range(4):
                    sl = ocp % 2
                    DMA("pool", "wd%d" % sl, wd[sl].ap,
                        wdn_d[l, :, ocp * 256:(ocp + 1) * 256].rearrange("(f p) n -> p f n", p=128), [], [wd[sl]])
                    for ol in range(2):
                        oc = ocp * 2 + ol
                        for (s0, n) in subs:
                            o = s0 - mt0
                            bk = nbank()
                            for f in range(NF):
                                mm(bk.ap[:, 0:n], wd[sl].ap[:, f, ol * 128:(ol + 1) * 128], act.ap[:, f, o:o + n],
                                   f == 0, f == NF - 1, [wd[sl], vs(act, f, o, o + n)], [bk])
                            hvs = hT_range(s0, n)
                            DVE(lambda e, bk=bk, oc=oc, s0=s0, n=n: e.tensor_tensor(
                                out=hT_t[:, oc, s0:s0 + n], in0=hT_t[:, oc, s0:s0 + n], in1=bk.ap[:, 0:n], op=ALU.add),
                                [bk] + hvs, hvs)

        Fz = Bump(0, SCR_BYTES)
        sqf = S.view(Fz.take([KC, 128], BF16), [KC, 128], BF16)
        rstdf = S.view(Fz.take([128], F32), [128], F32)
        oT = S.view(Fz.take([KC, 128], F32), [KC, 128], F32)
        ost = [S.view(Fz.take([1024], F32), [1024], F32) for _ in range(2)]
        for j in range(NT):
            p0 = j * 128
            lo, hi = max(p0, N_META), min(p0 + 128, T)
            if hi <= lo:
                continue
            hv = hT(j)
            rmsnorm([hv], slice(p0, p0 + 128), 128, FG, sqf, rstdf, oT, lambda kc: vs(oT, kc))
            og = ost[j % 2]
            for half in range(2):
                bk = nbank()
                for q in range(4):
                    kc = half * 4 + q
                    tr(bk.ap[:, q * 128:(q + 1) * 128], oT.ap[:, kc, :], idf, [oT], [bk])
                ACT(lambda e, bk=bk, og=og, half=half: e.activation(out=og.ap[:, half * 512:(half + 1) * 512],
                                                                    in_=bk.ap, func=AF.Copy), [bk], [og])
            ev = DMA("sp", "ost%d" % (j % 2), out_d[sq_i, lo - N_META:hi - N_META, :], og.ap[lo - p0:hi - p0, :], [og], [])
            P.final_events.append(ev)

    sem_names = list(ENGS[:4]) + sorted(set("dma:" + k for k in P.dma_cnt))
    sems = {}
    for i, nm in enumerate(sem_names):
        sems[nm] = es.enter_context(nc.semaphore("s%d" % i))
    fin = {}
    for (k, v) in P.final_events:
        fin[k] = max(fin.get(k, 0), v)
    block = es.enter_context(nc.Block())

    def run_stream(engname, e, extra_waits=()):
        for (waits, f, ev) in P.streams[engname]:
            for (k, val) in waits:
                e.wait_ge(sems[k], val)
            ins = getattr(e, f[0])(*f[1], **f[2])
            if ev[0].startswith("dma:"):
                ins.then_inc(sems[ev[0]], 16)
            else:
                ins.then_inc(sems[ev[0]], 1)
        for (k, val) in extra_waits:
            e.wait_ge(sems[k], val)

    @block.sync
    def _(e):
        run_stream("sp", e, list(fin.items()))

    @block.scalar
    def _(e):
        run_stream("act", e)

    @block.vector
    def _(e):
        run_stream("dve", e)

    @block.gpsimd
    def _(e):
        run_stream("pool", e)

    @block.tensor
    def _(e):
        run_stream("pe", e)

    es.close()
    return nc, {e: len(P.streams[e]) for e in ENGS}


def _consts():
    bf = ml_dtypes.bfloat16
    s = np.arange(128)[:, None]
    t = np.arange(128)[None, :]
    strict = (s < t).astype(np.float32)
    incl = (s <= t).astype(np.float32)
    blk = ((s // 64) == (t // 64)).astype(np.float32)
    rc0 = np.zeros((128, 4, 16), np.float32)
    for g, win in enumerate((2, 4, 8, 16)):
        rc0[:, g, :] = 1.0 / np.minimum(np.arange(16) + 1, win)[None, :]
    return {
        "c_ident_f": np.eye(128, dtype=np.float32),
        "c_ones_f": np.ones((128, 128), np.float32),
        "c_rc0": rc0.reshape(128, 64),
        "c_ident_b": np.eye(128, dtype=np.float32).astype(bf),
        "c_mask": np.concatenate([strict, incl], 1).astype(bf),
        "c_blk": blk.astype(bf),
        "c_o1k": np.full((128, 128), 1.0 / 1024, np.float32).astype(bf),
    }


def _pvec(inp, DEPTH):
    pv = np.zeros((128, DEPTH * NCOL + 8), np.float32)

    def put(col, vec, n):
        pv[:, col:col + n] = np.asarray(vec, np.float32).reshape(n, 128).T

    for l in range(DEPTH):
        b = l * NCOL
        put(b + 0, inp["ln1_g"][l], 8)
        put(b + 8, inp["ln2_g"][l], 8)
        mu = np.asarray(inp["mu_shift"][l], np.float32)
        put(b + 16, mu[0:1792], 14)
        pv[0:32, b + 30] = mu[1792:1824]
        if l >= 1:
            pv[32:64, b + 30] = np.asarray(inp["mu_vres"][l - 1], np.float32)
        put(b + 31, inp["pool_scale"][l], 4)
        put(b + 35, inp["w0"][l], 4)
        put(b + 39, inp["a0"][l], 4)
        put(b + 43, inp["k_k"][l], 4)
        put(b + 47, inp["k_a"][l], 4)
        put(b + 51, inp["r_k"][l], 4)
        if l >= 1:
            put(b + 55, inp["v0"][l - 1], 4)
        put(b + 59, inp["gn_w"][l], 4)
        put(b + 63, inp["gn_b"][l], 4)
    put(DEPTH * NCOL, inp["final_g"], 8)
    return pv


_CACHE = {}


def run(inputs, DEPTH, SEQ, BATCH, n_cores):
    NSEQ = BATCH // n_cores
    key = (DEPTH, SEQ, NSEQ)
    if key not in _CACHE:
        _CACHE[key] = build_program(DEPTH, SEQ, NSEQ)
    nc, counts = _CACHE[key]
    f32 = lambda a: np.ascontiguousarray(np.asarray(a, dtype=np.float32))
    shared = {k: f32(inputs[k]) for k in ("meta_tokens", "w_in", "pool_w", "w_lora_up", "a_lora_up", "g_lora_up",
                                          "w_out", "w_gate_up", "w_down")}
    if DEPTH > 1:
        shared["w_in_vres"] = f32(inputs["w_in_vres"])
        shared["v_lora_up"] = f32(inputs["v_lora_up"])
    else:
        shared["w_in_vres"] = np.zeros((1, D, 32), np.float32)
        shared["v_lora_up"] = np.zeros((1, 32, RW), np.float32)
    shared["pvec"] = _pvec(inputs, DEPTH)
    shared.update(_consts())
    x = f32(inputs["x"])
    in_maps = []
    for c in range(n_cores):
        m = dict(shared)
        m["x"] = np.ascontiguousarray(x[c * NSEQ:(c + 1) * NSEQ])
        in_maps.append(m)
    if DBG.get("trace"):
        res = run_bass_kernel_spmd(nc, in_maps, core_ids=list(range(n_cores)), trace=True)
        print("EXEC_TIME_NS", res.exec_time_ns, flush=True)
        DBG["res"] = res
    else:
        res = run_bass_kernel_spmd(nc, in_maps, core_ids=list(range(n_cores)))
    return np.concatenate([np.asarray(r["out"], dtype=np.float32) for r in res.results], axis=0)


def kernel(**inputs):
    return run(inputs, 4, 2048, 16, 8)
```

```python
import math
import numpy as np
import ml_dtypes
import concourse.bass as bass
import concourse.mybir as mybir
from concourse.bass_utils import run_bass_kernel_spmd

F32 = mybir.dt.float32
BF16 = mybir.dt.bfloat16
AF = mybir.ActivationFunctionType
ALU = mybir.AluOpType
AX = mybir.AxisListType

D = 1024
KC = 8
N_META = 16
POOLW = 512
RW = 512
N_IN = 2336
NINP = 2432
D_FF = 2816
NF = 22
RMS_EPS = 1e-6
GN_EPS = 64e-5
C0 = math.exp(-0.5)
NCOL = 72
ENGS = ("pe", "act", "dve", "pool", "sp")
DBG = {}


class View:
    def __init__(self, ap, res):
        self.ap = ap
        self.res = res


class Prog:
    def __init__(self, nc):
        self.nc = nc
        self.streams = {e: [] for e in ENGS}
        self.cnt = {e: 0 for e in ENGS}
        self.known = {e: {} for e in ENGS}
        self.keys = {}
        self.segs = [[0, 1 << 30, None, []]]
        self.dma_cnt = {}
        self.final_events = []

    def _states(self, res):
        if res[0] == "key":
            st = self.keys.get(res[1])
            if st is None:
                st = [None, []]
                self.keys[res[1]] = st
            return [st]
        _, s, e = res
        out = []
        segs = self.segs
        i = 0
        while i < len(segs):
            sg = segs[i]
            if sg[1] <= s or sg[0] >= e:
                i += 1
                continue
            if sg[0] < s:
                segs.insert(i, [sg[0], s, sg[2], list(sg[3])])
                sg[0] = s
                i += 1
                continue
            if sg[1] > e:
                segs.insert(i + 1, [e, sg[1], sg[2], list(sg[3])])
                sg[1] = e
            out.append(sg)
            i += 1
        return [_SegState(sg) for sg in out]

    def emit(self, eng, f, R=(), W=(), dma=None):
        deps = []
        for v in R:
            for st in self._states(v.res):
                w = st[0]
                if w is not None:
                    deps.append((w, True))
        for v in W:
            for st in self._states(v.res):
                w = st[0]
                if w is not None:
                    deps.append((w, False))
                for r in st[1]:
                    deps.append((r, False))
        waits = []
        kn = self.known[eng]
        need = {}
        for (ev, raw) in deps:
            k, val = ev
            if k == eng and eng == "pe":
                continue
            need[k] = max(need.get(k, 0), val)
        for k, val in need.items():
            if kn.get(k, 0) < val:
                kn[k] = val
                waits.append((k, val))
        if dma is None:
            self.cnt[eng] += 1
            ev = (eng, self.cnt[eng])
        else:
            self.dma_cnt[dma] = self.dma_cnt.get(dma, 0) + 16
            ev = ("dma:" + dma, self.dma_cnt[dma])
        rec = _Rec()
        f(rec)
        self.streams[eng].append((waits, rec.call, ev))
        for v in R:
            for st in self._states(v.res):
                st[1][:] = [r for r in st[1] if r[0] != ev[0]] + [ev]
        for v in W:
            for st in self._states(v.res):
                st[0] = ev
                st[1][:] = []
        return ev


class _Rec:
    def __getattr__(self, name):
        def m(*a, **k):
            self.call = (name, a, k)
            return self
        return m


class _SegState:
    def __init__(self, sg):
        self.sg = sg

    def __getitem__(self, i):
        return self.sg[2 + i]

    def __setitem__(self, i, v):
        self.sg[2 + i] = v


class Scratch:
    def __init__(self, ap_f32, nbytes):
        self.ap = ap_f32
        self.nbytes = nbytes

    def view(self, off, shape, dt):
        sz = 2 if dt == BF16 else 4
        n = int(np.prod(shape))
        nb = n * sz
        assert off % 4 == 0 and off + nb <= self.nbytes, (off, nb, self.nbytes)
        nb4 = (nb + 3) // 4
        a = self.ap[:, off // 4: off // 4 + nb4]
        if dt == BF16:
            a = a.bitcast(BF16)[:, 0:n]
        if len(shape) == 2:
            a = a.rearrange("p (a b) -> p a b", a=shape[0])
        elif len(shape) == 3:
            a = a.rearrange("p (a b c) -> p a b c", a=shape[0], b=shape[1])
        v = View(a, ("reg", off, off + nb4 * 4))
        v.off, v.shape, v.sz = off, tuple(shape), sz
        return v


def vs(v, i, c0=None, c1=None):
    A, Bn = v.shape
    c0 = 0 if c0 is None else c0
    c1 = Bn if c1 is None else c1
    lo = v.off + (i * Bn + c0) * v.sz
    hi = v.off + (i * Bn + c1) * v.sz
    return View(v.ap[:, i, c0:c1], ("reg", lo // 4 * 4, (hi + 3) // 4 * 4))


def vblk(v, i0, i1):
    A, Bn = v.shape
    lo = v.off + i0 * Bn * v.sz
    hi = v.off + i1 * Bn * v.sz
    return View(v.ap[:, i0:i1, :], ("reg", lo, hi))


class Bump:
    def __init__(self, base, limit):
        self.o = base
        self.limit = limit
        self.peak = base

    def take(self, shape, dt):
        sz = 2 if dt == BF16 else 4
        nb = (int(np.prod(shape)) * sz + 31) // 32 * 32
        o = self.o
        self.o += nb
        self.peak = max(self.peak, self.o)
        assert self.o <= self.limit, ("scratch overflow", self.o, self.limit)
        return o


def _ffn_tiling(T):
    nmac = -(-T // 1032)
    macs = []
    base = 0
    for m in range(nmac):
        ml = (T - base) // (nmac - m)
        nsub = -(-ml // 344)
        subs = []
        b2 = base
        for s in range(nsub):
            sl = (base + ml - b2) // (nsub - s)
            subs.append((b2, sl))
            b2 += sl
        macs.append(subs)
        base += ml
    return macs


def build_program(DEPTH, SEQ, NSEQ):
    T = N_META + SEQ
    NT = -(-T // 128)
    TP = NT * 128
    nc = bass.Bass("TRN2", target_bir_lowering=False)
    P = Prog(nc)

    def dram(name, shape, dt=F32, kind="ExternalInput"):
        return nc.dram_tensor(name, list(shape), dt, kind=kind).ap()

    LV = max(DEPTH - 1, 1)
    x_d = dram("x", [NSEQ, SEQ, D])
    meta_d = dram("meta_tokens", [N_META, D])
    w_in_d = dram("w_in", [DEPTH, D, N_IN])
    pool_w_d = dram("pool_w", [DEPTH, 4, 128, 128])
    wup_d = dram("w_lora_up", [DEPTH, 64, RW])
    aup_d = dram("a_lora_up", [DEPTH, 64, RW])
    gup_d = dram("g_lora_up", [DEPTH, 160, RW])
    vres_d = dram("w_in_vres", [LV, D, 32])
    vup_d = dram("v_lora_up", [LV, 32, RW])
    w_out_d = dram("w_out", [DEPTH, D, D])
    wgu_d = dram("w_gate_up", [DEPTH, D, 2 * D_FF])
    wdn_d = dram("w_down", [DEPTH, D_FF, D])
    pv_d = dram("pvec", [128, DEPTH * NCOL + 8])
    c_idf = dram("c_ident_f", [128, 128])
    c_onesf = dram("c_ones_f", [128, 128])
    c_rc0 = dram("c_rc0", [128, 4 * 16])
    c_idb = dram("c_ident_b", [128, 128], BF16)
    c_msk = dram("c_mask", [128, 256], BF16)
    c_blk = dram("c_blk", [128, 128], BF16)
    c_o1k = dram("c_o1k", [128, 128], BF16)
    out_d = dram("out", [NSEQ, SEQ, D], kind="ExternalOutput")

    import contextlib
    es = contextlib.ExitStack()

    def sb(name, shape, dt):
        return es.enter_context(nc.sbuf_tensor(name, list(shape), dt))

    def kv(name, ap):
        return View(ap, ("key", name))

    hT_t = sb("hT", [128, KC, TP], F32)
    vf_t = sb("vf", [128, 4, TP], BF16)
    pv = kv("pv", sb("pv", [128, DEPTH * NCOL + 8], F32)[:])
    idf = kv("idf", sb("idf", [128, 128], F32)[:])
    onesf = kv("onesf", sb("onesf", [128, 128], F32)[:])
    rc0 = kv("rc0", sb("rc0", [128, 4, 16], F32)[:])
    idb = kv("idb", sb("idb", [128, 128], BF16)[:])
    msk = kv("msk", sb("msk", [128, 256], BF16)[:])
    blk = kv("blk", sb("blk", [128, 128], BF16)[:])
    o1k = kv("o1k", sb("o1k", [128, 128], BF16)[:])
    wup = kv("wup", sb("wup", [128, RW], BF16)[:])
    aup = kv("aup", sb("aup", [128, RW], BF16)[:])
    gup0 = kv("gup0", sb("gup0", [128, RW], BF16)[:])
    gup1 = kv("gup1", sb("gup1", [128, RW], BF16)[:])
    vup = kv("vup", sb("vup", [128, RW], BF16)[:])
    plw = kv("plw", sb("plw", [128, 4, 128], BF16)[:])
    prevcol = kv("prevcol", sb("prevcol", [128, 16], F32)[:])
    Ubuf = kv("Ubuf", sb("Ubuf", [128, 4, 144], F32)[:])
    ST = kv("ST", sb("ST", [128, 4, 64], F32)[:])
    STb = kv("STb", sb("STb", [128, 4, 64], BF16)[:])
    rW = kv("rW", sb("rW", [128, 4, 129], F32)[:])
    LA = kv("LA", sb("LA", [128, 128], BF16)[:])
    LG0 = kv("LG0", sb("LG0", [128, 128], BF16)[:])
    LG1 = kv("LG1", sb("LG1", [128, 128], BF16)[:])
    SCR_BYTES = 106 * 1024
    scr_t = sb("scr", [128, SCR_BYTES // 4], F32)
    S = Scratch(scr_t[:], SCR_BYTES)
    psum_t = es.enter_context(nc.psum_tensor("ps", [128, 8, 512], F32))
    banks = [kv("bank%d" % i, psum_t[:, i, :]) for i in range(8)]
    bank_i = [0]

    def nbank():
        b = banks[bank_i[0] % 8]
        bank_i[0] += 1
        return b

    def hT(j):
        return kv("hT%d" % j, hT_t[:, :, j * 128:(j + 1) * 128])

    def hT_range(t0, n):
        return [kv("hT%d" % j, None) for j in range(t0 // 128, (t0 + n - 1) // 128 + 1)]

    def vf(j):
        return kv("vf%d" % j, vf_t[:, :, j * 128:(j + 1) * 128])

    def PE(f, R, W):
        return P.emit("pe", f, R, W)

    def ACT(f, R, W):
        return P.emit("act", f, R, W)

    def DVE(f, R, W):
        return P.emit("dve", f, R, W)

    def POOL(f, R, W):
        return P.emit("pool", f, R, W)

    def DMA(q, name, out_ap, in_ap, R, W):
        return P.emit(q, lambda e: e.dma_start(out=out_ap, in_=in_ap), R, W, dma=name)

    def mm(out_ap, lhsT, rhs, start, stop, R, W):
        PE(lambda e: e.matmul(out_ap, lhsT, rhs, start=start, stop=stop), R, W)

    def tr(out_ap, in_ap, ident, R, W):
        PE(lambda e: e.transpose(out_ap, in_ap, ident.ap), R + [ident], W)

    def bc(ap2, n):
        k = ap2.shape[1]
        return ap2.unsqueeze(2).to_broadcast([128, k, n])

    DMA("sp", "c0", pv.ap, pv_d, [], [pv])
    DMA("sp", "c1", idf.ap, c_idf, [], [idf])
    DMA("sp", "c2", onesf.ap, c_onesf, [], [onesf])
    DMA("sp", "c3", rc0.ap, c_rc0.rearrange("p (a b) -> p a b", a=4), [], [rc0])
    DMA("sp", "c4", idb.ap, c_idb, [], [idb])
    DMA("sp", "c5", msk.ap, c_msk, [], [msk])
    DMA("sp", "c6", blk.ap, c_blk, [], [blk])
    DMA("sp", "c7", o1k.ap, c_o1k, [], [o1k])
    for v in (wup, aup, gup0, gup1, vup, LA, LG0, LG1):
        POOL(lambda e, v=v: e.memset(v.ap, 0.0), [], [v])
    POOL(lambda e: e.memset(rW.ap, 1.0), [], [rW])

    def pcol(l, c0, n=1):
        return pv.ap[:, l * NCOL + c0: l * NCOL + c0 + n]

    FG = DEPTH * NCOL

    def rmsnorm(hviews, cols, n, gcol, sq, rstd, outv, out_slices):
        ACT(lambda e: e.activation(out=sq.ap[:, :, 0:n], in_=hT_t[:, :, cols], func=AF.Square), hviews, [sq])
        bk = nbank()
        for kc in range(KC):
            mm(bk.ap[:, 0:n], o1k.ap, sq.ap[:, kc, 0:n], kc == 0, kc == KC - 1, [sq, o1k], [bk])
        ACT(lambda e: e.activation(out=rstd.ap[:, 0:n], in_=bk.ap[:, 0:n], func=AF.Sqrt, bias=RMS_EPS, scale=1.0),
            [bk], [rstd])
        DVE(lambda e: e.reciprocal(out=rstd.ap[:, 0:n], in_=rstd.ap[:, 0:n]), [rstd], [rstd])
        for kc in range(KC):
            ov = out_slices(kc)
            DVE(lambda e, kc=kc, ov=ov: e.scalar_tensor_tensor(out=ov.ap, in0=hT_t[:, kc, cols],
                                                               scalar=pv.ap[:, gcol + kc: gcol + kc + 1],
                                                               in1=rstd.ap[:, 0:n], op0=ALU.mult, op1=ALU.mult),
                hviews + [rstd, pv], [ov])

    for sq_i in range(NSEQ):
        xs = [S.view(0, [1024], F32), S.view(4096, [1024], F32)]
        for j in range(NT):
            st = xs[j % 2]
            p0 = j * 128
            lo, hi = max(p0, N_META), min(p0 + 128, T)
            if j == 0 or hi < p0 + 128:
                POOL(lambda e, st=st: e.memset(st.ap, 0.0), [], [st])
            nm = "xs%d" % (j % 2)
            if j == 0:
                DMA("sp", nm, st.ap[0:N_META, :], meta_d, [], [st])
            if hi > lo:
                DMA("sp", nm, st.ap[lo - p0:hi - p0, :], x_d[sq_i, lo - N_META:hi - N_META, :], [], [st])
            hv = hT(j)
            for half in range(2):
                bk = nbank()
                for q in range(4):
                    kc = half * 4 + q
                    tr(bk.ap[:, q * 128:(q + 1) * 128], st.ap[:, kc * 128:(kc + 1) * 128], idf, [st], [bk])
                ACT(lambda e, bk=bk, half=half, j=j: e.activation(
                    out=hT_t[:, half * 4:half * 4 + 4, j * 128:(j + 1) * 128],
                    in_=bk.ap.rearrange("p (a b) -> p a b", a=4), func=AF.Copy), [bk], [hv])

        for l in range(DEPTH if not DBG.get('nolayers') else 0):
            B = Bump(0, SCR_BYTES)

            def V(shape, dt, off=None):
                return S.view(B.take(shape, dt) if off is None else off, shape, dt)

            gcols = [(0, 512), (512, 1024), (1024, 1536), (1536, 2048), (2048, NINP)]
            w_in_g = [V([KC, a1_ - a0_], BF16) for (a0_, a1_) in gcols]
            wout = V([KC, D], BF16)
            POOL(lambda e: e.memset(w_in_g[4].ap[:, :, N_IN - 2048:NINP - 2048], 0.0), [], [w_in_g[4]])
            for gi, (a0_, a1_) in enumerate(gcols):
                a1r = min(a1_, N_IN)
                DMA("pool", "win%d" % gi, w_in_g[gi].ap[:, :, 0:a1r - a0_],
                    w_in_d[l, :, a0_:a1r].rearrange("(k p) n -> p k n", p=128), [], [w_in_g[gi]])
            if l >= 1:
                DMA("pool", "win4", w_in_g[4].ap[:, :, N_IN - 2048:N_IN - 2048 + 32],
                    vres_d[l - 1].rearrange("(k p) n -> p k n", p=128), [], [w_in_g[4]])
            DMA("pool", "wout", wout.ap, w_out_d[l].rearrange("(k p) n -> p k n", p=128), [], [wout])
            DMA("pool", "wup", wup.ap[0:64, :], wup_d[l], [], [wup])
            DMA("pool", "aup", aup.ap[64:128, :], aup_d[l], [], [aup])
            DMA("pool", "gup0", gup0.ap, gup_d[l, 0:128, :], [], [gup0])
            DMA("pool", "gup1", gup1.ap[0:32, :], gup_d[l, 128:160, :], [], [gup1])
            if l >= 1:
                DMA("pool", "vup", vup.ap[32:64, :], vup_d[l - 1], [], [vup])
            DMA("pool", "plw", plw.ap, pool_w_d[l].rearrange("g c d -> c g d"), [], [plw])

            if DBG.get('gate'):
                gate = kv("gate", prevcol.ap[:, 15:16])
                POOL(lambda e: e.memset(gate.ap, 0.0), list(w_in_g) + [wout, wup, aup, gup0, gup1, vup, plw], [gate])
            ARe = V([4, 2, 128], BF16)
            ARo = V([4, 2, 128], BF16)
            BTe, BTo, KTe, KTo, BPe, BPo, KPe, KPo = [V([4, 128], BF16) for _ in range(8)]
            for v in (ARe, ARo, BTe, BTo, KTe, KTo, BPe, BPo, KPe, KPo):
                POOL(lambda e, v=v: e.memset(v.ap, 0.0), [], [v])
            cat = V([8, 128], BF16)
            bonus = V([4, 128], F32)
            gT = V([4, 128], F32)
            Vtok = V([8, 64], BF16)
            bpv = V([4, 128], BF16)
            kpv = V([4, 128], BF16)
            vb = V([4, 128], BF16)
            kksq = V([4, 128], BF16)
            stats = V([8, 8], F32)
            ncl = V([4], F32)
            m0 = B.o
            r_ = V([4, 128], F32)
            k_ = V([4, 128], F32)
            v_ = V([4, 128], F32)
            tmpA = V([4, 128], F32)
            m1 = B.o
            sqv = V([8, 128], BF16)
            hn = V([8, 128], BF16)
            rstd = V([128], F32)
            Pg = [V([4, 129], F32), V([4, 129], F32)]
            La = V([4, 144], F32)
            Lb = V([4, 144], F32)
            diff = V([4, 128], BF16)
            e1 = B.o
            B.o = m1
            sg = V([4, 128], F32)
            cl = V([4, 128], F32)
            iW = V([4, 128], F32)
            dW = V([4, 128], F32)
            a_ = V([4, 128], F32)
            kk = V([4, 128], F32)
            knew = V([4, 128], F32)
            b_ = V([4, 128], F32)
            tmpB = V([4, 128], F32)
            e2 = B.o
            B.o = m0
            Amat = V([8, 4, 128], BF16)
            TTf = V([8, 128], BF16)
            X0 = V([8, 128], BF16)
            Xp = V([8, 128], BF16)
            XTp = V([8, 128], BF16)
            TTp = V([8, 128], BF16)
            X1b = V([8, 64], BF16)
            Ub = V([8, 64], BF16)
            Ytok = V([8, 64], F32)
            Ysq = V([8, 64], F32)
            yn = V([8, 64], BF16)
            yg = V([4, 128], F32)
            B.o = max(B.o, e1, e2)

            g1c = l * NCOL
            POOL(lambda e: e.memset(prevcol.ap, 0.0), [], [prevcol])
            POOL(lambda e: e.memset(Ubuf.ap, 0.0), [], [Ubuf])
            POOL(lambda e: e.memset(ST.ap, 0.0), [], [ST])
            POOL(lambda e: e.memset(STb.ap, 0.0), [], [STb])

            for j in range(NT if not DBG.get('nomixer') else 0):
                cols = slice(j * 128, (j + 1) * 128)
                hv = hT(j)
                if j == 0:
                    rmsnorm([hv], cols, 128, g1c + 0, sqv, rstd, hn, lambda kc: vs(hn, kc))

                if DBG.get('stage', 99) < 1:
                    continue
                def inproj(gi, nchunk):
                    bk = nbank()
                    for ci in range(nchunk):
                        c = ci * 128
                        for kc in range(KC):
                            mm(bk.ap[:, ci * 128:(ci + 1) * 128], w_in_g[gi].ap[:, kc, c:c + 128], hn.ap[:, kc, :],
                               kc == 0, kc == KC - 1, [w_in_g[gi], hn], [bk])
                    return bk

                bk = inproj(0, 4)
                ACT(lambda e, bk=bk: e.activation(out=Ubuf.ap[:, :, 16:144],
                                                  in_=bk.ap.rearrange("p (a b) -> p a b", a=4), func=AF.Copy),
                    [bk], [Ubuf])

                def shift(gi, nchunk, outv):
                    bk = inproj(gi, nchunk)
                    pg = Pg[gi % 2]
                    pc = prevcol.ap[:, (gi - 1) * 4:(gi - 1) * 4 + nchunk]
                    POOL(lambda e: e.tensor_copy(out=pg.ap[:, 0:nchunk, 0:1], in_=pc.unsqueeze(2)), [prevcol], [pg])
                    ACT(lambda e: e.activation(out=pg.ap[:, 0:nchunk, 1:129],
                                               in_=bk.ap[:, 0:nchunk * 128].rearrange("p (a b) -> p a b", a=nchunk),
                                               func=AF.Copy), [bk], [pg])
                    POOL(lambda e: e.tensor_copy(out=pc.unsqueeze(2), in_=pg.ap[:, 0:nchunk, 128:129]), [pg], [prevcol])
                    mub = bc(pcol(l, 16 + (gi - 1) * 4, nchunk), 128)
                    DVE(lambda e: e.tensor_tensor(out=tmpA.ap[:, 0:nchunk, :], in0=pg.ap[:, 0:nchunk, 0:128],
                                                  in1=pg.ap[:, 0:nchunk, 1:129], op=ALU.subtract), [pg], [tmpA])
                    DVE(lambda e: e.tensor_tensor(out=tmpA.ap[:, 0:nchunk, :], in0=tmpA.ap[:, 0:nchunk, :],
                                                  in1=mub, op=ALU.mult), [tmpA, pv], [tmpA])
                    DVE(lambda e: e.tensor_tensor(out=outv.ap[:, 0:nchunk, :], in0=tmpA.ap[:, 0:nchunk, :],
                                                  in1=pg.ap[:, 0:nchunk, 1:129], op=ALU.add), [tmpA, pg], [outv])

                U = Ubuf.ap
                POOL(lambda e: e.tensor_tensor(out=La.ap[:, 0:4, 1:144], in0=U[:, 0:4, 1:144], in1=U[:, 0:4, 0:143],
                                               op=ALU.add), [Ubuf], [La])
                POOL(lambda e: e.tensor_tensor(out=Lb.ap[:, 1:4, 3:144], in0=La.ap[:, 1:4, 3:144],
                                               in1=La.ap[:, 1:4, 1:142], op=ALU.add), [La], [Lb])
                POOL(lambda e: e.tensor_tensor(out=La.ap[:, 2:4, 7:144], in0=Lb.ap[:, 2:4, 7:144],
                                               in1=Lb.ap[:, 2:4, 3:140], op=ALU.add), [Lb], [La])
                POOL(lambda e: e.tensor_tensor(out=Lb.ap[:, 3:4, 15:144], in0=La.ap[:, 3:4, 15:144],
                                               in1=La.ap[:, 3:4, 7:136], op=ALU.add), [La], [Lb])
                shift(4, 3, sg)
                ACT(lambda e: e.activation(out=LA.ap[0:64, :], in_=sg.ap[0:64, 0, :], func=AF.Tanh), [sg], [LA])
                ACT(lambda e: e.activation(out=LA.ap[64:128, :], in_=sg.ap[64:128, 0, :], func=AF.Copy), [sg], [LA])
                ACT(lambda e: e.activation(out=LG0.ap, in_=sg.ap[:, 1, :], func=AF.Sigmoid), [sg], [LG0])
                ACT(lambda e: e.activation(out=LG1.ap[0:32, :], in_=sg.ap[0:32, 2, :], func=AF.Sigmoid), [sg], [LG1])
                ACT(lambda e: e.activation(out=LG1.ap[32:64, :], in_=sg.ap[32:64, 2, :], func=AF.Copy), [sg], [LG1])

                shift(2, 4, k_)
                shift(1, 4, r_)
                shift(3, 4, v_)
                def lora4(lhs, rhsv, extra=None):
                    bk = nbank()
                    for hp in range(4):
                        if extra is None:
                            mm(bk.ap[:, hp * 128:(hp + 1) * 128], lhs.ap[:, hp * 128:(hp + 1) * 128], rhsv.ap,
                               True, True, [lhs, rhsv], [bk])
                        else:
                            mm(bk.ap[:, hp * 128:(hp + 1) * 128], lhs.ap[:, hp * 128:(hp + 1) * 128], rhsv.ap,
                               True, False, [lhs, rhsv], [bk])
                            mm(bk.ap[:, hp * 128:(hp + 1) * 128], extra[0].ap[:, hp * 128:(hp + 1) * 128], extra[1].ap,
                               False, True, [extra[0], extra[1]], [bk])
                    return bk

                bkw = lora4(wup, LA)
                bka = lora4(aup, LA)
                bkg = lora4(gup0, LG0, (gup1, LG1))
                if l >= 1:
                    bkv = lora4(vup, LG1)
                for g in range(4):
                    src = La if g % 2 == 0 else Lb
                    DVE(lambda e, g=g, src=src: e.scalar_tensor_tensor(
                        out=diff.ap[:, g, :], in0=src.ap[:, g, 16:144], scalar=1.0 / (2 << g),
                        in1=U[:, g, 16:144], op0=ALU.mult, op1=ALU.subtract), [src, Ubuf], [diff])
                    if j == 0:
                        DVE(lambda e, g=g, src=src: e.tensor_tensor(out=tmpA.ap[:, g, 0:16], in0=src.ap[:, g, 16:32],
                                                                    in1=rc0.ap[:, g, :], op=ALU.mult),
                            [src, rc0], [tmpA])
                        DVE(lambda e, g=g: e.tensor_tensor(out=diff.ap[:, g, 0:16], in0=tmpA.ap[:, g, 0:16],
                                                           in1=U[:, g, 16:32], op=ALU.subtract), [tmpA, Ubuf], [diff])
                POOL(lambda e: e.tensor_copy(out=U[:, :, 0:16], in_=U[:, :, 128:144]), [Ubuf, La, Lb, diff], [Ubuf])
                bk = nbank()
                for g in range(4):
                    mm(bk.ap[:, g * 128:(g + 1) * 128], plw.ap[:, g, :], diff.ap[:, g, :], True, True, [plw, diff], [bk])
                DVE(lambda e, bk=bk: e.tensor_tensor(out=cat.ap[:, 0:4, :], in0=bk.ap.rearrange("p (a b) -> p a b", a=4),
                                                     in1=bc(pcol(l, 31, 4), 128), op=ALU.mult), [bk, pv], [cat])

                for hp in range(4):
                    ACT(lambda e, hp=hp: e.activation(out=sg.ap[:, hp, :], in_=bkw.ap[:, hp * 128:(hp + 1) * 128],
                                                      func=AF.Sigmoid, bias=pcol(l, 35 + hp)), [bkw, pv], [vs(sg, hp)])
                for hp in range(4):
                    ACT(lambda e, hp=hp: e.activation(out=a_.ap[:, hp, :], in_=bka.ap[:, hp * 128:(hp + 1) * 128],
                                                      func=AF.Sigmoid, bias=pcol(l, 39 + hp)), [bka, pv], [vs(a_, hp)])
                ACT(lambda e: e.activation(out=gT.ap, in_=bkg.ap.rearrange("p (a b) -> p a b", a=4), func=AF.Copy),
                    [bkg], [gT])
                if l >= 1:
                    for hp in range(4):
                        ACT(lambda e, hp=hp: e.activation(out=tmpB.ap[:, hp, :], in_=bkv.ap[:, hp * 128:(hp + 1) * 128],
                                                          func=AF.Sigmoid, bias=pcol(l, 55 + hp)), [bkv, pv], [vs(tmpB, hp)])
                    vfj = vf(j)
                    DVE(lambda e: e.tensor_tensor(out=tmpA.ap, in0=vfj.ap, in1=v_.ap, op=ALU.subtract), [vfj, v_], [tmpA])
                    DVE(lambda e: e.tensor_tensor(out=tmpA.ap, in0=tmpA.ap, in1=tmpB.ap, op=ALU.mult), [tmpA, tmpB], [tmpA])
                    DVE(lambda e: e.tensor_tensor(out=v_.ap, in0=v_.ap, in1=tmpA.ap, op=ALU.add), [v_, tmpA], [v_])
                else:
                    vfj = vf(j)
                    POOL(lambda e: e.tensor_copy(out=vfj.ap, in_=v_.ap), [v_], [vfj])
                for hp in range(4):
                    DVE(lambda e, hp=hp: e.tensor_tensor_scan(out=cl.ap[:, hp, :], data0=onesf.ap, data1=sg.ap[:, hp, :],
                                                              initial=0.0, op0=ALU.mult, op1=ALU.add), [vs(sg, hp), onesf], [vs(cl, hp)])
                ACT(lambda e: e.activation(out=rW.ap[:, :, 1:129], in_=cl.ap, func=AF.Exp, scale=-C0), [cl], [rW])
                ACT(lambda e: e.activation(out=iW.ap, in_=cl.ap, func=AF.Exp, scale=C0), [cl], [iW])
                DVE(lambda e: e.tensor_scalar(out=ncl.ap.unsqueeze(2), in0=cl.ap[:, :, 127:128], scalar1=-C0, scalar2=None,
                                              op0=ALU.mult), [cl], [ncl])
                for hp in range(4):
                    ACT(lambda e, hp=hp: e.activation(out=dW.ap[:, hp, :], in_=cl.ap[:, hp, :], func=AF.Exp, scale=C0,
                                                      bias=ncl.ap[:, hp:hp + 1]), [vs(cl, hp), ncl], [vs(dW, hp)])
                DVE(lambda e: e.tensor_tensor(out=kk.ap, in0=k_.ap, in1=bc(pcol(l, 43, 4), 128), op=ALU.mult),
                    [k_, pv], [kk])
                POOL(lambda e: e.tensor_tensor(out=kksq.ap, in0=kk.ap, in1=kk.ap, op=ALU.mult), [kk], [kksq])
                bk = nbank()
                for hp in range(4):
                    mm(bk.ap[:, hp * 128:(hp + 1) * 128], blk.ap, kksq.ap[:, hp, :], True, True, [blk, kksq], [bk])
                ACT(lambda e, bk=bk: e.activation(out=tmpA.ap, in_=bk.ap.rearrange("p (a b) -> p a b", a=4),
                                                  func=AF.Sqrt, bias=1e-24, scale=1.0), [bk], [tmpA])
                DVE(lambda e: e.reciprocal(out=tmpA.ap, in_=tmpA.ap), [tmpA], [tmpA])
                DVE(lambda e: e.tensor_tensor(out=kk.ap, in0=kk.ap, in1=tmpA.ap, op=ALU.mult), [kk, tmpA], [kk])
                DVE(lambda e: e.scalar_tensor_tensor(out=tmpA.ap, in0=a_.ap, scalar=-1.0, in1=bc(pcol(l, 47, 4), 128),
                                                     op0=ALU.add, op1=ALU.mult), [a_, pv], [tmpA])
                DVE(lambda e: e.scalar_tensor_tensor(out=knew.ap, in0=tmpA.ap, scalar=1.0, in1=k_.ap,
                                                     op0=ALU.add, op1=ALU.mult), [tmpA, k_], [knew])
                POOL(lambda e: e.tensor_tensor(out=b_.ap, in0=kk.ap, in1=a_.ap, op=ALU.mult), [kk, a_], [b_])
                POOL(lambda e: e.tensor_tensor(out=tmpB.ap, in0=knew.ap, in1=bc(pcol(l, 51, 4), 128), op=ALU.mult),
                     [knew, pv], [tmpB])
                POOL(lambda e: e.tensor_tensor(out=kksq.ap, in0=tmpB.ap, in1=r_.ap, op=ALU.mult), [tmpB, r_], [kksq])
                bk = nbank()
                for hp in range(4):
                    mm(bk.ap[:, hp * 128:(hp + 1) * 128], blk.ap, kksq.ap[:, hp, :], True, True, [blk, kksq], [bk])
                DVE(lambda e, bk=bk: e.tensor_tensor(out=bonus.ap, in0=bk.ap.rearrange("p (a b) -> p a b", a=4),
                                                     in1=v_.ap, op=ALU.mult), [bk, v_], [bonus])
                for e_, (AR, BT, KT) in enumerate(((ARe, BTe, KTe), (ARo, BTo, KTo))):
                    ps_ = slice(64 * e_, 64 * e_ + 64)
                    DVE(lambda e, AR=AR, ps_=ps_: e.scalar_tensor_tensor(
                        out=AR.ap[ps_, :, 0, :], in0=kk.ap[ps_], scalar=-1.0, in1=rW.ap[ps_, :, 0:128],
                        op0=ALU.mult, op1=ALU.mult), [kk, rW], [AR])
                    POOL(lambda e, AR=AR, ps_=ps_: e.tensor_tensor(out=AR.ap[ps_, :, 1, :], in0=r_.ap[ps_],
                                                                   in1=rW.ap[ps_, :, 1:129], op=ALU.mult), [r_, rW], [AR])
                    DVE(lambda e, BT=BT, ps_=ps_: e.tensor_tensor(out=BT.ap[ps_], in0=b_.ap[ps_], in1=iW.ap[ps_],
                                                                  op=ALU.mult), [b_, iW], [BT])
                    POOL(lambda e, KT=KT, ps_=ps_: e.tensor_tensor(out=KT.ap[ps_], in0=knew.ap[ps_], in1=iW.ap[ps_],
                                                                   op=ALU.mult), [knew, iW], [KT])
                DVE(lambda e: e.tensor_tensor(out=bpv.ap, in0=b_.ap, in1=dW.ap, op=ALU.mult), [b_, dW], [bpv])
                POOL(lambda e: e.tensor_tensor(out=kpv.ap, in0=knew.ap, in1=dW.ap, op=ALU.mult), [knew, dW], [kpv])
                ACT(lambda e: e.activation(out=vb.ap, in_=v_.ap, func=AF.Copy), [v_], [vb])

                if DBG.get('stage', 99) < 4:
                    continue
                def tr4(src):
                    bk = nbank()
                    for hp in range(4):
                        mm(bk.ap[:, hp * 128:(hp + 1) * 128], src.ap[:, hp, :], idb.ap, True, True, [src, idb], [bk])
                    return bk, bk.ap.rearrange("p (a b) -> p a b", a=4)

                bk, bb = tr4(bpv)
                if DBG.get('sub', 9) < 2:
                    continue
                ACT(lambda e, bb=bb: e.activation(out=BPe.ap[:, :, 0:64], in_=bb[:, :, 0:64], func=AF.Copy), [bk], [BPe])
                if DBG.get('sub', 9) < 3:
                    continue
                ACT(lambda e, bb=bb: e.activation(out=BPo.ap[:, :, 64:128], in_=bb[:, :, 64:128], func=AF.Copy), [bk], [BPo])
                if DBG.get('sub', 9) < 4:
                    continue
                bk, bb = tr4(kpv)
                ACT(lambda e, bb=bb: e.activation(out=KPe.ap[:, :, 0:64], in_=bb[:, :, 0:64], func=AF.Copy), [bk], [KPe])
                ACT(lambda e, bb=bb: e.activation(out=KPo.ap[:, :, 64:128], in_=bb[:, :, 64:128], func=AF.Copy), [bk], [KPo])
                bk, bb = tr4(vb)
                ACT(lambda e, bb=bb: e.activation(out=Vtok.ap.rearrange("p a b -> p (a b)"),
                                                  in_=bb.rearrange("p a b -> p (a b)"), func=AF.Copy), [bk], [Vtok])

                if DBG.get('stage', 99) < 5:
                    continue
                for h in range(8):
                    hp, e_ = h // 2, h % 2
                    AR, BT, KT = (ARe, BTe, KTe) if e_ == 0 else (ARo, BTo, KTo)
                    bk = nbank()
                    rhs = AR.ap[:, hp, :, :].rearrange("p a b -> p (a b)")
                    mm(bk.ap[:, 0:256], BT.ap[:, hp, :], rhs, True, True, [BT, AR], [bk])
                    mm(bk.ap[:, 256:512], KT.ap[:, hp, :], rhs, True, True, [KT, AR], [bk])
                    DVE(lambda e, bk=bk, h=h: e.tensor_tensor(
                        out=Amat.ap[:, h, :, :].rearrange("p (a b) c -> p a b c", a=2),
                        in0=bk.ap.rearrange("p (a b c) -> p a b c", a=2, b=2),
                        in1=msk.ap.rearrange("p (b c) -> p b c", b=2).unsqueeze(1).to_broadcast([128, 2, 2, 128]),
                        op=ALU.mult), [bk, msk], [View(None, ("reg", Amat.off + h * 1024, Amat.off + (h + 1) * 1024))])
                for half in range(2):
                    bk = nbank()
                    for q in range(4):
                        mm(bk.ap[:, q * 128:(q + 1) * 128], Amat.ap[:, half * 4 + q, 0, :], idb.ap, True, True,
                           [Amat, idb], [bk])
                    ACT(lambda e, bk=bk, half=half: e.activation(
                        out=X0.ap[:, half * 4:half * 4 + 4, :].rearrange("p a b -> p (a b)"), in_=bk.ap, func=AF.Copy),
                        [bk], [X0])
                if DBG.get('stage', 99) < 6:
                    continue
                def blkv(v, hb):
                    return vblk(v, hb * 4, hb * 4 + 4)

                XTping = [View(Amat.ap[:, hb * 4:hb * 4 + 4, 0, :], Amat.res) for hb in range(2)]
                ping = ([blkv(X0, 0), blkv(X0, 1)], XTping, [blkv(TTf, 0), blkv(TTf, 1)])
                pong = ([blkv(Xp, 0), blkv(Xp, 1)], [blkv(XTp, 0), blkv(XTp, 1)], [blkv(TTp, 0), blkv(TTp, 1)])
                for hb in range(2):
                    DVE(lambda e, hb=hb: e.tensor_tensor(
                        out=ping[2][hb].ap, in0=ping[1][hb].ap, in1=idb.ap.unsqueeze(1).to_broadcast([128, 4, 128]),
                        op=ALU.add), [ping[1][hb], idb], [ping[2][hb]])
                real_tok = min(128, T - j * 128)
                NLV = 4 if real_tok <= 32 else 6
                for lvl in range(NLV):
                    (Xc, XTc, TTc), (Xn, XTn, TTn) = (ping, pong) if lvl % 2 == 0 else (pong, ping)
                    bkx = [nbank(), nbank()]
                    for hb in range(2):
                        for i in range(4):
                            mm(bkx[hb].ap[:, i * 128:(i + 1) * 128], XTc[hb].ap[:, i, :], Xc[hb].ap[:, i, :], True, True,
                               [XTc[hb], Xc[hb]], [bkx[hb]])
                    if lvl < NLV - 1:
                        bkt = [nbank(), nbank()]
                        for hb in range(2):
                            for i in range(4):
                                mm(bkt[hb].ap[:, i * 128:(i + 1) * 128], Xc[hb].ap[:, i, :], XTc[hb].ap[:, i, :], True, True,
                                   [XTc[hb], Xc[hb]], [bkt[hb]])
                    for hb in range(2):
                        ACT(lambda e, hb=hb: e.activation(out=Xn[hb].ap, in_=bkx[hb].ap.rearrange("p (a b) -> p a b", a=4),
                                                          func=AF.Copy), [bkx[hb]], [Xn[hb]])
                    if lvl < NLV - 1:
                        for hb in range(2):
                            DVE(lambda e, hb=hb: e.tensor_scalar(out=XTn[hb].ap,
                                                                 in0=bkt[hb].ap.rearrange("p (a b) -> p a b", a=4),
                                                                 scalar1=1.0, scalar2=None, op0=ALU.mult), [bkt[hb]], [XTn[hb]])
                    bkT = [nbank(), nbank()]
                    for hb in range(2):
                        for i in range(4):
                            o = bkT[hb].ap[:, i * 128:(i + 1) * 128]
                            mm(o, Xn[hb].ap[:, i, :], TTc[hb].ap[:, i, :], True, True, [Xn[hb], TTc[hb]], [bkT[hb]])
                    for hb in range(2):
                        DVE(lambda e, hb=hb: e.tensor_tensor(out=TTn[hb].ap,
                                                             in0=bkT[hb].ap.rearrange("p (a b) -> p a b", a=4),
                                                             in1=TTc[hb].ap, op=ALU.add), [bkT[hb], TTc[hb]], [TTn[hb]])

                if DBG.get('stage', 99) < 7:
                    continue
                def pad(e_):
                    return (ARe, BPe, KPe) if e_ == 0 else (ARo, BPo, KPo)

                bk1 = nbank()
                for h in range(8):
                    hp, e_ = h // 2, h % 2
                    AR = pad(e_)[0]
                    o = bk1.ap[:, h * 64:(h + 1) * 64]
                    mm(o, AR.ap[:, hp, 0, :], STb.ap[:, hp, :], True, False, [AR, STb], [bk1])
                    mm(o, Amat.ap[:, h, 2, :], Vtok.ap[:, h, :], False, True, [Amat, Vtok], [bk1])
                ACT(lambda e: e.activation(out=X1b.ap.rearrange("p a b -> p (a b)"), in_=bk1.ap, func=AF.Copy), [bk1], [X1b])
                bk2 = nbank()
                for h in range(8):
                    mm(bk2.ap[:, h * 64:(h + 1) * 64], TTf.ap[:, h, :], X1b.ap[:, h, :], True, True, [TTf, X1b], [bk2])
                ACT(lambda e: e.activation(out=Ub.ap.rearrange("p a b -> p (a b)"), in_=bk2.ap, func=AF.Copy), [bk2], [Ub])
                bk3 = nbank()
                for h in range(8):
                    hp, e_ = h // 2, h % 2
                    AR = pad(e_)[0]
                    o = bk3.ap[:, h * 64:(h + 1) * 64]
                    mm(o, AR.ap[:, hp, 1, :], STb.ap[:, hp, :], True, False, [AR, STb], [bk3])
                    mm(o, Amat.ap[:, h, 1, :], Ub.ap[:, h, :], False, False, [Amat, Ub], [bk3])
                    mm(o, Amat.ap[:, h, 3, :], Vtok.ap[:, h, :], False, True, [Amat, Vtok], [bk3])
                bk4 = nbank()
                for hp in range(4):
                    o = bk4.ap[:, hp * 64:(hp + 1) * 64]
                    for e_ in range(2):
                        h = 2 * hp + e_
                        _, BPx, KPx = pad(e_)
                        mm(o, BPx.ap[:, hp, :], Ub.ap[:, h, :], e_ == 0, False, [BPx, Ub], [bk4])
                        mm(o, KPx.ap[:, hp, :], Vtok.ap[:, h, :], False, e_ == 1, [KPx, Vtok], [bk4])
                for hp in range(4):
                    DVE(lambda e, hp=hp: e.scalar_tensor_tensor(out=ST.ap[:, hp, :], in0=ST.ap[:, hp, :],
                                                                scalar=rW.ap[:, hp, 128:129],
                                                                in1=bk4.ap[:, hp * 64:(hp + 1) * 64],
                                                                op0=ALU.mult, op1=ALU.add), [ST, rW, bk4], [ST])
                ACT(lambda e: e.activation(out=STb.ap, in_=ST.ap, func=AF.Copy), [ST], [STb])

                if DBG.get('stage', 99) < 8:
                    continue
                ACT(lambda e: e.activation(out=Ytok.ap.rearrange("p a b -> p (a b)"), in_=bk3.ap, func=AF.Copy), [bk3], [Ytok])
                s1 = stats.ap[:, 0, :]
                s2 = stats.ap[:, 1, :]
                mean = stats.ap[:, 2, :]
                msq = stats.ap[:, 3, :]
                var = stats.ap[:, 4, :]
                rs = stats.ap[:, 5, :]
                DVE(lambda e: e.tensor_reduce(out=s1, in_=Ytok.ap, axis=AX.X, op=ALU.add), [Ytok], [stats])
                POOL(lambda e: e.tensor_tensor(out=Ysq.ap, in0=Ytok.ap, in1=Ytok.ap, op=ALU.mult), [Ytok], [Ysq])
                DVE(lambda e: e.tensor_reduce(out=s2, in_=Ysq.ap, axis=AX.X, op=ALU.add), [Ysq, stats], [stats])
                DVE(lambda e: e.tensor_scalar(out=mean, in0=s1, scalar1=1.0 / 64, scalar2=None, op0=ALU.mult), [stats], [stats])
                DVE(lambda e: e.tensor_tensor(out=msq, in0=mean, in1=mean, op=ALU.mult), [stats], [stats])
                DVE(lambda e: e.scalar_tensor_tensor(out=var, in0=s2, scalar=1.0 / 64, in1=msq, op0=ALU.mult,
                                                     op1=ALU.subtract), [stats], [stats])
                ACT(lambda e: e.activation(out=rs, in_=var, func=AF.Sqrt, bias=GN_EPS, scale=1.0), [stats], [stats])
                DVE(lambda e: e.reciprocal(out=rs, in_=rs), [stats], [stats])
                DVE(lambda e: e.tensor_tensor(out=Ysq.ap, in0=Ytok.ap, in1=mean.unsqueeze(2).to_broadcast([128, 8, 64]),
                                              op=ALU.subtract), [Ytok, stats, Ysq], [Ysq])
                DVE(lambda e: e.tensor_tensor(out=yn.ap, in0=Ysq.ap, in1=rs.unsqueeze(2).to_broadcast([128, 8, 64]),
                                              op=ALU.mult), [Ysq, stats], [yn])
                bk = nbank()
                bb = bk.ap
                ynf = yn.ap.rearrange("p a b -> p (a b)")
                for hp in range(4):
                    mm(bb[:, hp * 128:(hp + 1) * 128], ynf[:, hp * 128:(hp + 1) * 128], idb.ap, True, True, [yn, idb], [bk])
                for hp in range(4):
                    ACT(lambda e, hp=hp, bb=bb: e.activation(out=yg.ap[:, hp, :], in_=bb[:, hp * 128:(hp + 1) * 128],
                                                             func=AF.Identity, scale=pcol(l, 59 + hp),
                                                             bias=pcol(l, 63 + hp)), [bk, pv], [vs(yg, hp)])
                DVE(lambda e: e.tensor_tensor(out=yg.ap, in0=yg.ap, in1=bonus.ap, op=ALU.add), [yg, bonus], [yg])
                DVE(lambda e: e.tensor_tensor(out=cat.ap[:, 4:8, :], in0=yg.ap, in1=gT.ap, op=ALU.mult), [yg, gT], [cat])

                if DBG.get('stage', 99) < 9:
                    continue
                if j + 1 < NT:
                    rmsnorm([hT(j + 1)], slice((j + 1) * 128, (j + 2) * 128), 128, g1c + 0, sqv, rstd, hn,
                            lambda kc: vs(hn, kc))
                for half in range(2):
                    bk = nbank()
                    for q in range(4):
                        oc = half * 4 + q
                        for kc in range(KC):
                            mm(bk.ap[:, q * 128:(q + 1) * 128], wout.ap[:, kc, oc * 128:(oc + 1) * 128], cat.ap[:, kc, :],
                               kc == 0, kc == KC - 1, [wout, cat], [bk])
                    DVE(lambda e, bk=bk, half=half, cols=cols: e.tensor_tensor(
                        out=hT_t[:, half * 4:half * 4 + 4, cols], in0=hT_t[:, half * 4:half * 4 + 4, cols],
                        in1=bk.ap.rearrange("p (a b) -> p a b", a=4), op=ALU.add), [bk, hv], [hv])

            g2c = l * NCOL + 8
            for subs in (_ffn_tiling(T) if not DBG.get('noffn') else []):
                F = Bump(0, SCR_BYTES)
                mt0 = subs[0][0]
                ml = sum(n for _, n in subs)
                act = S.view(F.take([NF, 1032], BF16), [NF, 1032], BF16)
                hn2 = S.view(F.take([KC, 1032], BF16), [KC, 1032], BF16)
                sq2 = S.view(F.take([KC, 128], BF16), [KC, 128], BF16)
                rstd2 = S.view(F.take([128], F32), [128], F32)
                sil = [S.view(F.take([344], F32), [344], F32) for _ in range(2)]
                wg = [S.view(F.take([KC, 256], BF16), [KC, 256], BF16) for _ in range(2)]
                wu = [S.view(F.take([KC, 256], BF16), [KC, 256], BF16) for _ in range(2)]
                wd = [S.view(F.take([NF, 256], BF16), [NF, 256], BF16) for _ in range(2)]
                pc0 = mt0
                while pc0 < mt0 + ml:
                    pn = min((pc0 // 128 + 1) * 128, mt0 + ml) - pc0
                    o = pc0 - mt0
                    rmsnorm(hT_range(pc0, pn), slice(pc0, pc0 + pn), pn, g2c, sq2, rstd2, hn2,
                            lambda kc, o=o, pn=pn: vs(hn2, kc, o, o + pn))
                    pc0 += pn
                for fp in range(NF // 2):
                    sl = fp % 2
                    DMA("pool", "wg%d" % sl, wg[sl].ap,
                        wgu_d[l, :, fp * 256:(fp + 1) * 256].rearrange("(k p) n -> p k n", p=128), [], [wg[sl]])
                    DMA("pool", "wu%d" % sl, wu[sl].ap,
                        wgu_d[l, :, D_FF + fp * 256:D_FF + (fp + 1) * 256].rearrange("(k p) n -> p k n", p=128),
                        [], [wu[sl]])
                    for fl in range(2):
                        f = fp * 2 + fl
                        for si, (s0, n) in enumerate(subs):
                            o = s0 - mt0
                            bg = nbank()
                            bu = nbank()
                            for kc in range(KC):
                                mm(bg.ap[:, 0:n], wg[sl].ap[:, kc, fl * 128:(fl + 1) * 128], hn2.ap[:, kc, o:o + n],
                                   kc == 0, kc == KC - 1, [wg[sl], hn2], [bg])
                            for kc in range(KC):
                                mm(bu.ap[:, 0:n], wu[sl].ap[:, kc, fl * 128:(fl + 1) * 128], hn2.ap[:, kc, o:o + n],
                                   kc == 0, kc == KC - 1, [wu[sl], hn2], [bu])
                            sv = sil[(f * 3 + si) % 2]
                            ACT(lambda e, bg=bg, sv=sv, n=n: e.activation(out=sv.ap[:, 0:n], in_=bg.ap[:, 0:n], func=AF.Silu),
                                [bg], [sv])
                            DVE(lambda e, bu=bu, sv=sv, n=n, f=f, o=o: e.tensor_tensor(
                                out=act.ap[:, f, o:o + n], in0=bu.ap[:, 0:n], in1=sv.ap[:, 0:n], op=ALU.mult),
                                [bu, sv], [vs(act, f, o, o + n)])
                for ocp in range(4):
                    sl = ocp % 2
                    DMA("pool", "wd%d" % sl, wd[sl].ap,
                        wdn_d[l, :, ocp * 256:(ocp + 1) * 256].rearrange("(f p) n -> p f n", p=128), [], [wd[sl]])
                    for ol in range(2):
                        oc = ocp * 2 + ol
                        for (s0, n) in subs:
                            o = s0 - mt0
                            bk = nbank()
                            for f in range(NF):
                                mm(bk.ap[:, 0:n], wd[sl].ap[:, f, ol * 128:(ol + 1) * 128], act.ap[:, f, o:o + n],
                                   f == 0, f == NF - 1, [wd[sl], vs(act, f, o, o + n)], [bk])
                            hvs = hT_range(s0, n)
                            DVE(lambda e, bk=bk, oc=oc, s0=s0, n=n: e.tensor_tensor(
                                out=hT_t[:, oc, s0:s0 + n], in0=hT_t[:, oc, s0:s0 + n], in1=bk.ap[:, 0:n], op=ALU.add),
                                [bk] + hvs, hvs)

        Fz = Bump(0, SCR_BYTES)
        sqf = S.view(Fz.take([KC, 128], BF16), [KC, 128], BF16)
        rstdf = S.view(Fz.take([128], F32), [128], F32)
        oT = S.view(Fz.take([KC, 128], F32), [KC, 128], F32)
        ost = [S.view(Fz.take([1024], F32), [1024], F32) for _ in range(2)]
        for j in range(NT):
            p0 = j * 128
            lo, hi = max(p0, N_META), min(p0 + 128, T)
            if hi <= lo:
                continue
            hv = hT(j)
            rmsnorm([hv], slice(p0, p0 + 128), 128, FG, sqf, rstdf, oT, lambda kc: vs(oT, kc))
            og = ost[j % 2]
            for half in range(2):
                bk = nbank()
                for q in range(4):
                    kc = half * 4 + q
                    tr(bk.ap[:, q * 128:(q + 1) * 128], oT.ap[:, kc, :], idf, [oT], [bk])
                ACT(lambda e, bk=bk, og=og, half=half: e.activation(out=og.ap[:, half * 512:(half + 1) * 512],
                                                                    in_=bk.ap, func=AF.Copy), [bk], [og])
            ev = DMA("sp", "ost%d" % (j % 2), out_d[sq_i, lo - N_META:hi - N_META, :], og.ap[lo - p0:hi - p0, :], [og], [])
            P.final_events.append(ev)

    sem_names = list(ENGS[:4]) + sorted(set("dma:" + k for k in P.dma_cnt))
    sems = {}
    for i, nm in enumerate(sem_names):
        sems[nm] = es.enter_context(nc.semaphore("s%d" % i))
    fin = {}
    for (k, v) in P.final_events:
        fin[k] = max(fin.get(k, 0), v)
    block = es.enter_context(nc.Block())

    def run_stream(engname, e, extra_waits=()):
        for (waits, f, ev) in P.streams[engname]:
            for (k, val) in waits:
                e.wait_ge(sems[k], val)
            ins = getattr(e, f[0])(*f[1], **f[2])
            if ev[0].startswith("dma:"):
                ins.then_inc(sems[ev[0]], 16)
            else:
                ins.then_inc(sems[ev[0]], 1)
        for (k, val) in extra_waits:
            e.wait_ge(sems[k], val)

    @block.sync
    def _(e):
        run_stream("sp", e, list(fin.items()))

    @block.scalar
    def _(e):
        run_stream("act", e)

    @block.vector
    def _(e):
        run_stream("dve", e)

    @block.gpsimd
    def _(e):
        run_stream("pool", e)

    @block.tensor
    def _(e):
        run_stream("pe", e)

    es.close()
    return nc, {e: len(P.streams[e]) for e in ENGS}


def _consts():
    bf = ml_dtypes.bfloat16
    s = np.arange(128)[:, None]
    t = np.arange(128)[None, :]
    strict = (s < t).astype(np.float32)
    incl = (s <= t).astype(np.float32)
    blk = ((s // 64) == (t // 64)).astype(np.float32)
    rc0 = np.zeros((128, 4, 16), np.float32)
    for g, win in enumerate((2, 4, 8, 16)):
        rc0[:, g, :] = 1.0 / np.minimum(np.arange(16) + 1, win)[None, :]
    return {
        "c_ident_f": np.eye(128, dtype=np.float32),
        "c_ones_f": np.ones((128, 128), np.float32),
        "c_rc0": rc0.reshape(128, 64),
        "c_ident_b": np.eye(128, dtype=np.float32).astype(bf),
        "c_mask": np.concatenate([strict, incl], 1).astype(bf),
        "c_blk": blk.astype(bf),
        "c_o1k": np.full((128, 128), 1.0 / 1024, np.float32).astype(bf),
    }


def _pvec(inp, DEPTH):
    pv = np.zeros((128, DEPTH * NCOL + 8), np.float32)

    def put(col, vec, n):
        pv[:, col:col + n] = np.asarray(vec, np.float32).reshape(n, 128).T

    for l in range(DEPTH):
        b = l * NCOL
        put(b + 0, inp["ln1_g"][l], 8)
        put(b + 8, inp["ln2_g"][l], 8)
        mu = np.asarray(inp["mu_shift"][l], np.float32)
        put(b + 16, mu[0:1792], 14)
        pv[0:32, b + 30] = mu[1792:1824]
        if l >= 1:
            pv[32:64, b + 30] = np.asarray(inp["mu_vres"][l - 1], np.float32)
        put(b + 31, inp["pool_scale"][l], 4)
        put(b + 35, inp["w0"][l], 4)
        put(b + 39, inp["a0"][l], 4)
        put(b + 43, inp["k_k"][l], 4)
        put(b + 47, inp["k_a"][l], 4)
        put(b + 51, inp["r_k"][l], 4)
        if l >= 1:
            put(b + 55, inp["v0"][l - 1], 4)
        put(b + 59, inp["gn_w"][l], 4)
        put(b + 63, inp["gn_b"][l], 4)
    put(DEPTH * NCOL, inp["final_g"], 8)
    return pv


_CACHE = {}


def run(inputs, DEPTH, SEQ, BATCH, n_cores):
    NSEQ = BATCH // n_cores
    key = (DEPTH, SEQ, NSEQ)
    if key not in _CACHE:
        _CACHE[key] = build_program(DEPTH, SEQ, NSEQ)
    nc, counts = _CACHE[key]
    f32 = lambda a: np.ascontiguousarray(np.asarray(a, dtype=np.float32))
    shared = {k: f32(inputs[k]) for k in ("meta_tokens", "w_in", "pool_w", "w_lora_up", "a_lora_up", "g_lora_up",
                                          "w_out", "w_gate_up", "w_down")}
    if DEPTH > 1:
        shared["w_in_vres"] = f32(inputs["w_in_vres"])
        shared["v_lora_up"] = f32(inputs["v_lora_up"])
    else:
        shared["w_in_vres"] = np.zeros((1, D, 32), np.float32)
        shared["v_lora_up"] = np.zeros((1, 32, RW), np.float32)
    shared["pvec"] = _pvec(inputs, DEPTH)
    shared.update(_consts())
    x = f32(inputs["x"])
    in_maps = []
    for c in range(n_cores):
        m = dict(shared)
        m["x"] = np.ascontiguousarray(x[c * NSEQ:(c + 1) * NSEQ])
        in_maps.append(m)
    if DBG.get("trace"):
        res = run_bass_kernel_spmd(nc, in_maps, core_ids=list(range(n_cores)), trace=True)
        print("EXEC_TIME_NS", res.exec_time_ns, flush=True)
        DBG["res"] = res
    else:
        res = run_bass_kernel_spmd(nc, in_maps, core_ids=list(range(n_cores)))
    return np.concatenate([np.asarray(r["out"], dtype=np.float32) for r in res.results], axis=0)


def kernel(**inputs):
    return run(inputs, 4, 2048, 16, 8)
```

```python
import math
import numpy as np
import ml_dtypes
import concourse.bass as bass
import concourse.mybir as mybir
from concourse.bass_utils import run_bass_kernel_spmd

F32 = mybir.dt.float32
BF16 = mybir.dt.bfloat16
AF = mybir.ActivationFunctionType
ALU = mybir.AluOpType
AX = mybir.AxisListType

D = 1024
KC = 8
N_META = 16
POOLW = 512
RW = 512
N_IN = 2336
NINP = 2432
D_FF = 2816
NF = 22
RMS_EPS = 1e-6
GN_EPS = 64e-5
C0 = math.exp(-0.5)
NCOL = 72
ENGS = ("pe", "act", "dve", "pool", "sp")
DBG = {}


class View:
    def __init__(self, ap, res):
        self.ap = ap
        self.res = res


class Prog:
    def __init__(self, nc):
        self.nc = nc
        self.streams = {e: [] for e in ENGS}
        self.cnt = {e: 0 for e in ENGS}
        self.known = {e: {} for e in ENGS}
        self.keys = {}
        self.segs = [[0, 1 << 30, None, []]]
        self.dma_cnt = {}
        self.final_events = []

    def _states(self, res):
        if res[0] == "key":
            st = self.keys.get(res[1])
            if st is None:
                st = [None, []]
                self.keys[res[1]] = st
            return [st]
        _, s, e = res
        out = []
        segs = self.segs
        i = 0
        while i < len(segs):
            sg = segs[i]
            if sg[1] <= s or sg[0] >= e:
                i += 1
                continue
            if sg[0] < s:
                segs.insert(i, [sg[0], s, sg[2], list(sg[3])])
                sg[0] = s
                i += 1
                continue
            if sg[1] > e:
                segs.insert(i + 1, [e, sg[1], sg[2], list(sg[3])])
                sg[1] = e
            out.append(sg)
            i += 1
        return [_SegState(sg) for sg in out]

    def emit(self, eng, f, R=(), W=(), dma=None):
        deps = []
        for v in R:
            for st in self._states(v.res):
                w = st[0]
                if w is not None:
                    deps.append((w, True))
        for v in W:
            for st in self._states(v.res):
                w = st[0]
                if w is not None:
                    deps.append((w, False))
                for r in st[1]:
                    deps.append((r, False))
        waits = []
        kn = self.known[eng]
        need = {}
        for (ev, raw) in deps:
            k, val = ev
            if k == eng and eng == "pe":
                continue
            need[k] = max(need.get(k, 0), val)
        for k, val in need.items():
            if kn.get(k, 0) < val:
                kn[k] = val
                waits.append((k, val))
        if dma is None:
            self.cnt[eng] += 1
            ev = (eng, self.cnt[eng])
        else:
            self.dma_cnt[dma] = self.dma_cnt.get(dma, 0) + 16
            ev = ("dma:" + dma, self.dma_cnt[dma])
        rec = _Rec()
        f(rec)
        self.streams[eng].append((waits, rec.call, ev))
        for v in R:
            for st in self._states(v.res):
                st[1][:] = [r for r in st[1] if r[0] != ev[0]] + [ev]
        for v in W:
            for st in self._states(v.res):
                st[0] = ev
                st[1][:] = []
        return ev


class _Rec:
    def __getattr__(self, name):
        def m(*a, **k):
            self.call = (name, a, k)
            return self
        return m


class _SegState:
    def __init__(self, sg):
        self.sg = sg

    def __getitem__(self, i):
        return self.sg[2 + i]

    def __setitem__(self, i, v):
        self.sg[2 + i] = v


class Scratch:
    def __init__(self, ap_f32, nbytes):
        self.ap = ap_f32
        self.nbytes = nbytes

    def view(self, off, shape, dt):
        sz = 2 if dt == BF16 else 4
        n = int(np.prod(shape))
        nb = n * sz
        assert off % 4 == 0 and off + nb <= self.nbytes, (off, nb, self.nbytes)
        nb4 = (nb + 3) // 4
        a = self.ap[:, off // 4: off // 4 + nb4]
        if dt == BF16:
            a = a.bitcast(BF16)[:, 0:n]
        if len(shape) == 2:
            a = a.rearrange("p (a b) -> p a b", a=shape[0])
        elif len(shape) == 3:
            a = a.rearrange("p (a b c) -> p a b c", a=shape[0], b=shape[1])
        v = View(a, ("reg", off, off + nb4 * 4))
        v.off, v.shape, v.sz = off, tuple(shape), sz
        return v


def vs(v, i, c0=None, c1=None):
    A, Bn = v.shape
    c0 = 0 if c0 is None else c0
    c1 = Bn if c1 is None else c1
    lo = v.off + (i * Bn + c0) * v.sz
    hi = v.off + (i * Bn + c1) * v.sz
    return View(v.ap[:, i, c0:c1], ("reg", lo // 4 * 4, (hi + 3) // 4 * 4))


def vblk(v, i0, i1):
    A, Bn = v.shape
    lo = v.off + i0 * Bn * v.sz
    hi = v.off + i1 * Bn * v.sz
    return View(v.ap[:, i0:i1, :], ("reg", lo, hi))


class Bump:
    def __init__(self, base, limit):
        self.o = base
        self.limit = limit
        self.peak = base

    def take(self, shape, dt):
        sz = 2 if dt == BF16 else 4
        nb = (int(np.prod(shape)) * sz + 31) // 32 * 32
        o = self.o
        self.o += nb
        self.peak = max(self.peak, self.o)
        assert self.o <= self.limit, ("scratch overflow", self.o, self.limit)
        return o


def _ffn_tiling(T):
    nmac = -(-T // 1032)
    macs = []
    base = 0
    for m in range(nmac):
        ml = (T - base) // (nmac - m)
        nsub = -(-ml // 344)
        subs = []
        b2 = base
        for s in range(nsub):
            sl = (base + ml - b2) // (nsub - s)
            subs.append((b2, sl))
            b2 += sl
        macs.append(subs)
        base += ml
    return macs


def build_program(DEPTH, SEQ, NSEQ):
    T = N_META + SEQ
    NT = -(-T // 128)
    TP = NT * 128
    nc = bass.Bass("TRN2", target_bir_lowering=False)
    P = Prog(nc)

    def dram(name, shape, dt=F32, kind="ExternalInput"):
        return nc.dram_tensor(name, list(shape), dt, kind=kind).ap()

    LV = max(DEPTH - 1, 1)
    x_d = dram("x", [NSEQ, SEQ, D])
    meta_d = dram("meta_tokens", [N_META, D])
    w_in_d = dram("w_in", [DEPTH, D, N_IN])
    pool_w_d = dram("pool_w", [DEPTH, 4, 128, 128])
    wup_d = dram("w_lora_up", [DEPTH, 64, RW])
    aup_d = dram("a_lora_up", [DEPTH, 64, RW])
    gup_d = dram("g_lora_up", [DEPTH, 160, RW])
    vres_d = dram("w_in_vres", [LV, D, 32])
    vup_d = dram("v_lora_up", [LV, 32, RW])
    w_out_d = dram("w_out", [DEPTH, D, D])
    wgu_d = dram("w_gate_up", [DEPTH, D, 2 * D_FF])
    wdn_d = dram("w_down", [DEPTH, D_FF, D])
    pv_d = dram("pvec", [128, DEPTH * NCOL + 8])
    c_idf = dram("c_ident_f", [128, 128])
    c_onesf = dram("c_ones_f", [128, 128])
    c_rc0 = dram("c_rc0", [128, 4 * 16])
    c_idb = dram("c_ident_b", [128, 128], BF16)
    c_msk = dram("c_mask", [128, 256], BF16)
    c_blk = dram("c_blk", [128, 128], BF16)
    c_o1k = dram("c_o1k", [128, 128], BF16)
    out_d = dram("out", [NSEQ, SEQ, D], kind="ExternalOutput")

    import contextlib
    es = contextlib.ExitStack()

    def sb(name, shape, dt):
        return es.enter_context(nc.sbuf_tensor(name, list(shape), dt))

    def kv(name, ap):
        return View(ap, ("key", name))

    hT_t = sb("hT", [128, KC, TP], F32)
    vf_t = sb("vf", [128, 4, TP], BF16)
    pv = kv("pv", sb("pv", [128, DEPTH * NCOL + 8], F32)[:])
    idf = kv("idf", sb("idf", [128, 128], F32)[:])
    onesf = kv("onesf", sb("onesf", [128, 128], F32)[:])
    rc0 = kv("rc0", sb("rc0", [128, 4, 16], F32)[:])
    idb = kv("idb", sb("idb", [128, 128], BF16)[:])
    msk = kv("msk", sb("msk", [128, 256], BF16)[:])
    blk = kv("blk", sb("blk", [128, 128], BF16)[:])
    o1k = kv("o1k", sb("o1k", [128, 128], BF16)[:])
    wup = kv("wup", sb("wup", [128, RW], BF16)[:])
    aup = kv("aup", sb("aup", [128, RW], BF16)[:])
    gup0 = kv("gup0", sb("gup0", [128, RW], BF16)[:])
    gup1 = kv("gup1", sb("gup1", [128, RW], BF16)[:])
    vup = kv("vup", sb("vup", [128, RW], BF16)[:])
    plw = kv("plw", sb("plw", [128, 4, 128], BF16)[:])
    prevcol = kv("prevcol", sb("prevcol", [128, 16], F32)[:])
    Ubuf = kv("Ubuf", sb("Ubuf", [128, 4, 144], F32)[:])
    ST = kv("ST", sb("ST", [128, 4, 64], F32)[:])
    STb = kv("STb", sb("STb", [128, 4, 64], BF16)[:])
    rW = kv("rW", sb("rW", [128, 4, 129], F32)[:])
    LA = kv("LA", sb("LA", [128, 128], BF16)[:])
    LG0 = kv("LG0", sb("LG0", [128, 128], BF16)[:])
    LG1 = kv("LG1", sb("LG1", [128, 128], BF16)[:])
    SCR_BYTES = 106 * 1024
    scr_t = sb("scr", [128, SCR_BYTES // 4], F32)
    S = Scratch(scr_t[:], SCR_BYTES)
    psum_t = es.enter_context(nc.psum_tensor("ps", [128, 8, 512], F32))
    banks = [kv("bank%d" % i, psum_t[:, i, :]) for i in range(8)]
    bank_i = [0]

    def nbank():
        b = banks[bank_i[0] % 8]
        bank_i[0] += 1
        return b

    def hT(j):
        return kv("hT%d" % j, hT_t[:, :, j * 128:(j + 1) * 128])

    def hT_range(t0, n):
        return [kv("hT%d" % j, None) for j in range(t0 // 128, (t0 + n - 1) // 128 + 1)]

    def vf(j):
        return kv("vf%d" % j, vf_t[:, :, j * 128:(j + 1) * 128])

    def PE(f, R, W):
        return P.emit("pe", f, R, W)

    def ACT(f, R, W):
        return P.emit("act", f, R, W)

    def DVE(f, R, W):
        return P.emit("dve", f, R, W)

    def POOL(f, R, W):
        return P.emit("pool", f, R, W)

    def DMA(q, name, out_ap, in_ap, R, W):
        return P.emit(q, lambda e: e.dma_start(out=out_ap, in_=in_ap), R, W, dma=name)

    def mm(out_ap, lhsT, rhs, start, stop, R, W):
        PE(lambda e: e.matmul(out_ap, lhsT, rhs, start=start, stop=stop), R, W)

    def tr(out_ap, in_ap, ident, R, W):
        PE(lambda e: e.transpose(out_ap, in_ap, ident.ap), R + [ident], W)

    def bc(ap2, n):
        k = ap2.shape[1]
        return ap2.unsqueeze(2).to_broadcast([128, k, n])

    DMA("sp", "c0", pv.ap, pv_d, [], [pv])
    DMA("sp", "c1", idf.ap, c_idf, [], [idf])
    DMA("sp", "c2", onesf.ap, c_onesf, [], [onesf])
    DMA("sp", "c3", rc0.ap, c_rc0.rearrange("p (a b) -> p a b", a=4), [], [rc0])
    DMA("sp", "c4", idb.ap, c_idb, [], [idb])
    DMA("sp", "c5", msk.ap, c_msk, [], [msk])
    DMA("sp", "c6", blk.ap, c_blk, [], [blk])
    DMA("sp", "c7", o1k.ap, c_o1k, [], [o1k])
    for v in (wup, aup, gup0, gup1, vup, LA, LG0, LG1):
        POOL(lambda e, v=v: e.memset(v.ap, 0.0), [], [v])
    POOL(lambda e: e.memset(rW.ap, 1.0), [], [rW])

    def pcol(l, c0, n=1):
        return pv.ap[:, l * NCOL + c0: l * NCOL + c0 + n]

    FG = DEPTH * NCOL

    def rmsnorm(hviews, cols, n, gcol, sq, rstd, outv, out_slices):
        ACT(lambda e: e.activation(out=sq.ap[:, :, 0:n], in_=hT_t[:, :, cols], func=AF.Square), hviews, [sq])
        bk = nbank()
        for kc in range(KC):
            mm(bk.ap[:, 0:n], o1k.ap, sq.ap[:, kc, 0:n], kc == 0, kc == KC - 1, [sq, o1k], [bk])
        ACT(lambda e: e.activation(out=rstd.ap[:, 0:n], in_=bk.ap[:, 0:n], func=AF.Sqrt, bias=RMS_EPS, scale=1.0),
            [bk], [rstd])
        DVE(lambda e: e.reciprocal(out=rstd.ap[:, 0:n], in_=rstd.ap[:, 0:n]), [rstd], [rstd])
        for kc in range(KC):
            ov = out_slices(kc)
            DVE(lambda e, kc=kc, ov=ov: e.scalar_tensor_tensor(out=ov.ap, in0=hT_t[:, kc, cols],
                                                               scalar=pv.ap[:, gcol + kc: gcol + kc + 1],
                                                               in1=rstd.ap[:, 0:n], op0=ALU.mult, op1=ALU.mult),
                hviews + [rstd, pv], [ov])

    for sq_i in range(NSEQ):
        xs = [S.view(0, [1024], F32), S.view(4096, [1024], F32)]
        for j in range(NT):
            st = xs[j % 2]
            p0 = j * 128
            lo, hi = max(p0, N_META), min(p0 + 128, T)
            if j == 0 or hi < p0 + 128:
                POOL(lambda e, st=st: e.memset(st.ap, 0.0), [], [st])
            nm = "xs%d" % (j % 2)
            if j == 0:
                DMA("sp", nm, st.ap[0:N_META, :], meta_d, [], [st])
            if hi > lo:
                DMA("sp", nm, st.ap[lo - p0:hi - p0, :], x_d[sq_i, lo - N_META:hi - N_META, :], [], [st])
            hv = hT(j)
            for half in range(2):
                bk = nbank()
                for q in range(4):
                    kc = half * 4 + q
                    tr(bk.ap[:, q * 128:(q + 1) * 128], st.ap[:, kc * 128:(kc + 1) * 128], idf, [st], [bk])
                ACT(lambda e, bk=bk, half=half, j=j: e.activation(
                    out=hT_t[:, half * 4:half * 4 + 4, j * 128:(j + 1) * 128],
                    in_=bk.ap.rearrange("p (a b) -> p a b", a=4), func=AF.Copy), [bk], [hv])

        for l in range(DEPTH if not DBG.get('nolayers') else 0):
            B = Bump(0, SCR_BYTES)

            def V(shape, dt, off=None):
                return S.view(B.take(shape, dt) if off is None else off, shape, dt)

            gcols = [(0, 512), (512, 1024), (1024, 1536), (1536, 2048), (2048, NINP)]
            w_in_g = [V([KC, a1_ - a0_], BF16) for (a0_, a1_) in gcols]
            wout = V([KC, D], BF16)
            POOL(lambda e: e.memset(w_in_g[4].ap[:, :, N_IN - 2048:NINP - 2048], 0.0), [], [w_in_g[4]])
            for gi, (a0_, a1_) in enumerate(gcols):
                a1r = min(a1_, N_IN)
                DMA("pool", "win%d" % gi, w_in_g[gi].ap[:, :, 0:a1r - a0_],
                    w_in_d[l, :, a0_:a1r].rearrange("(k p) n -> p k n", p=128), [], [w_in_g[gi]])
            if l >= 1:
                DMA("pool", "win4", w_in_g[4].ap[:, :, N_IN - 2048:N_IN - 2048 + 32],
                    vres_d[l - 1].rearrange("(k p) n -> p k n", p=128), [], [w_in_g[4]])
            DMA("pool", "wout", wout.ap, w_out_d[l].rearrange("(k p) n -> p k n", p=128), [], [wout])
            DMA("pool", "wup", wup.ap[0:64, :], wup_d[l], [], [wup])
            DMA("pool", "aup", aup.ap[64:128, :], aup_d[l], [], [aup])
            DMA("pool", "gup0", gup0.ap, gup_d[l, 0:128, :], [], [gup0])
            DMA("pool", "gup1", gup1.ap[0:32, :], gup_d[l, 128:160, :], [], [gup1])
            if l >= 1:
                DMA("pool", "vup", vup.ap[32:64, :], vup_d[l - 1], [], [vup])
            DMA("pool", "plw", plw.ap, pool_w_d[l].rearrange("g c d -> c g d"), [], [plw])

            if DBG.get('gate'):
                gate = kv("gate", prevcol.ap[:, 15:16])
                POOL(lambda e: e.memset(gate.ap, 0.0), list(w_in_g) + [wout, wup, aup, gup0, gup1, vup, plw], [gate])
            ARe = V([4, 2, 128], BF16)
            ARo = V([4, 2, 128], BF16)
            BTe, BTo, KTe, KTo, BPe, BPo, KPe, KPo = [V([4, 128], BF16) for _ in range(8)]
            for v in (ARe, ARo, BTe, BTo, KTe, KTo, BPe, BPo, KPe, KPo):
                POOL(lambda e, v=v: e.memset(v.ap, 0.0), [], [v])
            cat = V([8, 128], BF16)
            bonus = V([4, 128], F32)
            gT = V([4, 128], F32)
            Vtok = V([8, 64], BF16)
            bpv = V([4, 128], BF16)
            kpv = V([4, 128], BF16)
            vb = V([4, 128], BF16)
            kksq = V([4, 128], BF16)
            stats = V([8, 8], F32)
            ncl = V([4], F32)
            m0 = B.o
            r_ = V([4, 128], F32)
            k_ = V([4, 128], F32)
            v_ = V([4, 128], F32)
            tmpA = V([4, 128], F32)
            m1 = B.o
            sqv = V([8, 128], BF16)
            hn = V([8, 128], BF16)
            rstd = V([128], F32)
            Pg = [V([4, 129], F32), V([4, 129], F32)]
            La = V([4, 144], F32)
            Lb = V([4, 144], F32)
            diff = V([4, 128], BF16)
            e1 = B.o
            B.o = m1
            sg = V([4, 128], F32)
            cl = V([4, 128], F32)
            iW = V([4, 128], F32)
            dW = V([4, 128], F32)
            a_ = V([4, 128], F32)
            kk = V([4, 128], F32)
            knew = V([4, 128], F32)
            b_ = V([4, 128], F32)
            tmpB = V([4, 128], F32)
            e2 = B.o
            B.o = m0
            Amat = V([8, 4, 128], BF16)
            TTf = V([8, 128], BF16)
            X0 = V([8, 128], BF16)
            Xp = V([8, 128], BF16)
            XTp = V([8, 128], BF16)
            TTp = V([8, 128], BF16)
            X1b = V([8, 64], BF16)
            Ub = V([8, 64], BF16)
            Ytok = V([8, 64], F32)
            Ysq = V([8, 64], F32)
            yn = V([8, 64], BF16)
            yg = V([4, 128], F32)
            B.o = max(B.o, e1, e2)

            g1c = l * NCOL
            POOL(lambda e: e.memset(prevcol.ap, 0.0), [], [prevcol])
            POOL(lambda e: e.memset(Ubuf.ap, 0.0), [], [Ubuf])
            POOL(lambda e: e.memset(ST.ap, 0.0), [], [ST])
            POOL(lambda e: e.memset(STb.ap, 0.0), [], [STb])

            for j in range(NT if not DBG.get('nomixer') else 0):
                cols = slice(j * 128, (j + 1) * 128)
                hv = hT(j)
                if j == 0:
                    rmsnorm([hv], cols, 128, g1c + 0, sqv, rstd, hn, lambda kc: vs(hn, kc))

                if DBG.get('stage', 99) < 1:
                    continue
                def inproj(gi, nchunk):
                    bk = nbank()
                    for ci in range(nchunk):
                        c = ci * 128
                        for kc in range(KC):
                            mm(bk.ap[:, ci * 128:(ci + 1) * 128], w_in_g[gi].ap[:, kc, c:c + 128], hn.ap[:, kc, :],
                               kc == 0, kc == KC - 1, [w_in_g[gi], hn], [bk])
                    return bk

                bk = inproj(0, 4)
                ACT(lambda e, bk=bk: e.activation(out=Ubuf.ap[:, :, 16:144],
                                                  in_=bk.ap.rearrange("p (a b) -> p a b", a=4), func=AF.Copy),
                    [bk], [Ubuf])

                def shift(gi, nchunk, outv):
                    bk = inproj(gi, nchunk)
                    pg = Pg[gi % 2]
                    pc = prevcol.ap[:, (gi - 1) * 4:(gi - 1) * 4 + nchunk]
                    POOL(lambda e: e.tensor_copy(out=pg.ap[:, 0:nchunk, 0:1], in_=pc.unsqueeze(2)), [prevcol], [pg])
                    ACT(lambda e: e.activation(out=pg.ap[:, 0:nchunk, 1:129],
                                               in_=bk.ap[:, 0:nchunk * 128].rearrange("p (a b) -> p a b", a=nchunk),
                                               func=AF.Copy), [bk], [pg])
                    POOL(lambda e: e.tensor_copy(out=pc.unsqueeze(2), in_=pg.ap[:, 0:nchunk, 128:129]), [pg], [prevcol])
                    mub = bc(pcol(l, 16 + (gi - 1) * 4, nchunk), 128)
                    DVE(lambda e: e.tensor_tensor(out=tmpA.ap[:, 0:nchunk, :], in0=pg.ap[:, 0:nchunk, 0:128],
                                                  in1=pg.ap[:, 0:nchunk, 1:129], op=ALU.subtract), [pg], [tmpA])
                    DVE(lambda e: e.tensor_tensor(out=tmpA.ap[:, 0:nchunk, :], in0=tmpA.ap[:, 0:nchunk, :],
                                                  in1=mub, op=ALU.mult), [tmpA, pv], [tmpA])
                    DVE(lambda e: e.tensor_tensor(out=outv.ap[:, 0:nchunk, :], in0=tmpA.ap[:, 0:nchunk, :],
                                                  in1=pg.ap[:, 0:nchunk, 1:129], op=ALU.add), [tmpA, pg], [outv])

                U = Ubuf.ap
                POOL(lambda e: e.tensor_tensor(out=La.ap[:, 0:4, 1:144], in0=U[:, 0:4, 1:144], in1=U[:, 0:4, 0:143],
                                               op=ALU.add), [Ubuf], [La])
                POOL(lambda e: e.tensor_tensor(out=Lb.ap[:, 1:4, 3:144], in0=La.ap[:, 1:4, 3:144],
                                               in1=La.ap[:, 1:4, 1:142], op=ALU.add), [La], [Lb])
                POOL(lambda e: e.tensor_tensor(out=La.ap[:, 2:4, 7:144], in0=Lb.ap[:, 2:4, 7:144],
                                               in1=Lb.ap[:, 2:4, 3:140], op=ALU.add), [Lb], [La])
                POOL(lambda e: e.tensor_tensor(out=Lb.ap[:, 3:4, 15:144], in0=La.ap[:, 3:4, 15:144],
                                               in1=La.ap[:, 3:4, 7:136], op=ALU.add), [La], [Lb])
                shift(4, 3, sg)
                ACT(lambda e: e.activation(out=LA.ap[0:64, :], in_=sg.ap[0:64, 0, :], func=AF.Tanh), [sg], [LA])
                ACT(lambda e: e.activation(out=LA.ap[64:128, :], in_=sg.ap[64:128, 0, :], func=AF.Copy), [sg], [LA])
                ACT(lambda e: e.activation(out=LG0.ap, in_=sg.ap[:, 1, :], func=AF.Sigmoid), [sg], [LG0])
                ACT(lambda e: e.activation(out=LG1.ap[0:32, :], in_=sg.ap[0:32, 2, :], func=AF.Sigmoid), [sg], [LG1])
                ACT(lambda e: e.activation(out=LG1.ap[32:64, :], in_=sg.ap[32:64, 2, :], func=AF.Copy), [sg], [LG1])

                shift(2, 4, k_)
                shift(1, 4, r_)
                shift(3, 4, v_)
                def lora4(lhs, rhsv, extra=None):
                    bk = nbank()
                    for hp in range(4):
                        if extra is None:
                            mm(bk.ap[:, hp * 128:(hp + 1) * 128], lhs.ap[:, hp * 128:(hp + 1) * 128], rhsv.ap,
                               True, True, [lhs, rhsv], [bk])
                        else:
                            mm(bk.ap[:, hp * 128:(hp + 1) * 128], lhs.ap[:, hp * 128:(hp + 1) * 128], rhsv.ap,
                               True, False, [lhs, rhsv], [bk])
                            mm(bk.ap[:, hp * 128:(hp + 1) * 128], extra[0].ap[:, hp * 128:(hp + 1) * 128], extra[1].ap,
                               False, True, [extra[0], extra[1]], [bk])
                    return bk

                bkw = lora4(wup, LA)
                bka = lora4(aup, LA)
                bkg = lora4(gup0, LG0, (gup1, LG1))
                if l >= 1:
                    bkv = lora4(vup, LG1)
                for g in range(4):
                    src = La if g % 2 == 0 else Lb
                    DVE(lambda e, g=g, src=src: e.scalar_tensor_tensor(
                        out=diff.ap[:, g, :], in0=src.ap[:, g, 16:144], scalar=1.0 / (2 << g),
                        in1=U[:, g, 16:144], op0=ALU.mult, op1=ALU.subtract), [src, Ubuf], [diff])
                    if j == 0:
                        DVE(lambda e, g=g, src=src: e.tensor_tensor(out=tmpA.ap[:, g, 0:16], in0=src.ap[:, g, 16:32],
                                                                    in1=rc0.ap[:, g, :], op=ALU.mult),
                            [src, rc0], [tmpA])
                        DVE(lambda e, g=g: e.tensor_tensor(out=diff.ap[:, g, 0:16], in0=tmpA.ap[:, g, 0:16],
                                                           in1=U[:, g, 16:32], op=ALU.subtract), [tmpA, Ubuf], [diff])
                POOL(lambda e: e.tensor_copy(out=U[:, :, 0:16], in_=U[:, :, 128:144]), [Ubuf, La, Lb, diff], [Ubuf])
                bk = nbank()
                for g in range(4):
                    mm(bk.ap[:, g * 128:(g + 1) * 128], plw.ap[:, g, :], diff.ap[:, g, :], True, True, [plw, diff], [bk])
                DVE(lambda e, bk=bk: e.tensor_tensor(out=cat.ap[:, 0:4, :], in0=bk.ap.rearrange("p (a b) -> p a b", a=4),
                                                     in1=bc(pcol(l, 31, 4), 128), op=ALU.mult), [bk, pv], [cat])

                for hp in range(4):
                    ACT(lambda e, hp=hp: e.activation(out=sg.ap[:, hp, :], in_=bkw.ap[:, hp * 128:(hp + 1) * 128],
                                                      func=AF.Sigmoid, bias=pcol(l, 35 + hp)), [bkw, pv], [vs(sg, hp)])
                for hp in range(4):
                    ACT(lambda e, hp=hp: e.activation(out=a_.ap[:, hp, :], in_=bka.ap[:, hp * 128:(hp + 1) * 128],
                                                      func=AF.Sigmoid, bias=pcol(l, 39 + hp)), [bka, pv], [vs(a_, hp)])
                ACT(lambda e: e.activation(out=gT.ap, in_=bkg.ap.rearrange("p (a b) -> p a b", a=4), func=AF.Copy),
                    [bkg], [gT])
                if l >= 1:
                    for hp in range(4):
                        ACT(lambda e, hp=hp: e.activation(out=tmpB.ap[:, hp, :], in_=bkv.ap[:, hp * 128:(hp + 1) * 128],
                                                          func=AF.Sigmoid, bias=pcol(l, 55 + hp)), [bkv, pv], [vs(tmpB, hp)])
                    vfj = vf(j)
                    DVE(lambda e: e.tensor_tensor(out=tmpA.ap, in0=vfj.ap, in1=v_.ap, op=ALU.subtract), [vfj, v_], [tmpA])
                    DVE(lambda e: e.tensor_tensor(out=tmpA.ap, in0=tmpA.ap, in1=tmpB.ap, op=ALU.mult), [tmpA, tmpB], [tmpA])
                    DVE(lambda e: e.tensor_tensor(out=v_.ap, in0=v_.ap, in1=tmpA.ap, op=ALU.add), [v_, tmpA], [v_])
                else:
                    vfj = vf(j)
                    POOL(lambda e: e.tensor_copy(out=vfj.ap, in_=v_.ap), [v_], [vfj])
                for hp in range(4):
                    DVE(lambda e, hp=hp: e.tensor_tensor_scan(out=cl.ap[:, hp, :], data0=onesf.ap, data1=sg.ap[:, hp, :],
                                                              initial=0.0, op0=ALU.mult, op1=ALU.add), [vs(sg, hp), onesf], [vs(cl, hp)])
                ACT(lambda e: e.activation(out=rW.ap[:, :, 1:129], in_=cl.ap, func=AF.Exp, scale=-C0), [cl], [rW])
                ACT(lambda e: e.activation(out=iW.ap, in_=cl.ap, func=AF.Exp, scale=C0), [cl], [iW])
                DVE(lambda e: e.tensor_scalar(out=ncl.ap.unsqueeze(2), in0=cl.ap[:, :, 127:128], scalar1=-C0, scalar2=None,
                                              op0=ALU.mult), [cl], [ncl])
                for hp in range(4):
                    ACT(lambda e, hp=hp: e.activation(out=dW.ap[:, hp, :], in_=cl.ap[:, hp, :], func=AF.Exp, scale=C0,
                                                      bias=ncl.ap[:, hp:hp + 1]), [vs(cl, hp), ncl], [vs(dW, hp)])
                DVE(lambda e: e.tensor_tensor(out=kk.ap, in0=k_.ap, in1=bc(pcol(l, 43, 4), 128), op=ALU.mult),
                    [k_, pv], [kk])
                POOL(lambda e: e.tensor_tensor(out=kksq.ap, in0=kk.ap, in1=kk.ap, op=ALU.mult), [kk], [kksq])
                bk = nbank()
                for hp in range(4):
                    mm(bk.ap[:, hp * 128:(hp + 1) * 128], blk.ap, kksq.ap[:, hp, :], True, True, [blk, kksq], [bk])
                ACT(lambda e, bk=bk: e.activation(out=tmpA.ap, in_=bk.ap.rearrange("p (a b) -> p a b", a=4),
                                                  func=AF.Sqrt, bias=1e-24, scale=1.0), [bk], [tmpA])
                DVE(lambda e: e.reciprocal(out=tmpA.ap, in_=tmpA.ap), [tmpA], [tmpA])
                DVE(lambda e: e.tensor_tensor(out=kk.ap, in0=kk.ap, in1=tmpA.ap, op=ALU.mult), [kk, tmpA], [kk])
                DVE(lambda e: e.scalar_tensor_tensor(out=tmpA.ap, in0=a_.ap, scalar=-1.0, in1=bc(pcol(l, 47, 4), 128),
                                                     op0=ALU.add, op1=ALU.mult), [a_, pv], [tmpA])
                DVE(lambda e: e.scalar_tensor_tensor(out=knew.ap, in0=tmpA.ap, scalar=1.0, in1=k_.ap,
                                                     op0=ALU.add, op1=ALU.mult), [tmpA, k_], [knew])
                POOL(lambda e: e.tensor_tensor(out=b_.ap, in0=kk.ap, in1=a_.ap, op=ALU.mult), [kk, a_], [b_])
                POOL(lambda e: e.tensor_tensor(out=tmpB.ap, in0=knew.ap, in1=bc(pcol(l, 51, 4), 128), op=ALU.mult),
                     [knew, pv], [tmpB])
                POOL(lambda e: e.tensor_tensor(out=kksq.ap, in0=tmpB.ap, in1=r_.ap, op=ALU.mult), [tmpB, r_], [kksq])
                bk = nbank()
                for hp in range(4):
                    mm(bk.ap[:, hp * 128:(hp + 1) * 128], blk.ap, kksq.ap[:, hp, :], True, True, [blk, kksq], [bk])
                DVE(lambda e, bk=bk: e.tensor_tensor(out=bonus.ap, in0=bk.ap.rearrange("p (a b) -> p a b", a=4),
                                                     in1=v_.ap, op=ALU.mult), [bk, v_], [bonus])
                for e_, (AR, BT, KT) in enumerate(((ARe, BTe, KTe), (ARo, BTo, KTo))):
                    ps_ = slice(64 * e_, 64 * e_ + 64)
                    DVE(lambda e, AR=AR, ps_=ps_: e.scalar_tensor_tensor(
                        out=AR.ap[ps_, :, 0, :], in0=kk.ap[ps_], scalar=-1.0, in1=rW.ap[ps_, :, 0:128],
                        op0=ALU.mult, op1=ALU.mult), [kk, rW], [AR])
                    POOL(lambda e, AR=AR, ps_=ps_: e.tensor_tensor(out=AR.ap[ps_, :, 1, :], in0=r_.ap[ps_],
                                                                   in1=rW.ap[ps_, :, 1:129], op=ALU.mult), [r_, rW], [AR])
                    DVE(lambda e, BT=BT, ps_=ps_: e.tensor_tensor(out=BT.ap[ps_], in0=b_.ap[ps_], in1=iW.ap[ps_],
                                                                  op=ALU.mult), [b_, iW], [BT])
                    POOL(lambda e, KT=KT, ps_=ps_: e.tensor_tensor(out=KT.ap[ps_], in0=knew.ap[ps_], in1=iW.ap[ps_],
                                                                   op=ALU.mult), [knew, iW], [KT])
                DVE(lambda e: e.tensor_tensor(out=bpv.ap, in0=b_.ap, in1=dW.ap, op=ALU.mult), [b_, dW], [bpv])
                POOL(lambda e: e.tensor_tensor(out=kpv.ap, in0=knew.ap, in1=dW.ap, op=ALU.mult), [knew, dW], [kpv])
                ACT(lambda e: e.activation(out=vb.ap, in_=v_.ap, func=AF.Copy), [v_], [vb])

                if DBG.get('stage', 99) < 4:
                    continue
                def tr4(src):
                    bk = nbank()
                    for hp in range(4):
                        mm(bk.ap[:, hp * 128:(hp + 1) * 128], src.ap[:, hp, :], idb.ap, True, True, [src, idb], [bk])
                    return bk, bk.ap.rearrange("p (a b) -> p a b", a=4)

                bk, bb = tr4(bpv)
                if DBG.get('sub', 9) < 2:
                    continue
                ACT(lambda e, bb=bb: e.activation(out=BPe.ap[:, :, 0:64], in_=bb[:, :, 0:64], func=AF.Copy), [bk], [BPe])
                if DBG.get('sub', 9) < 3:
                    continue
                ACT(lambda e, bb=bb: e.activation(out=BPo.ap[:, :, 64:128], in_=bb[:, :, 64:128], func=AF.Copy), [bk], [BPo])
                if DBG.get('sub', 9) < 4:
                    continue
                bk, bb = tr4(kpv)
                ACT(lambda e, bb=bb: e.activation(out=KPe.ap[:, :, 0:64], in_=bb[:, :, 0:64], func=AF.Copy), [bk], [KPe])
                ACT(lambda e, bb=bb: e.activation(out=KPo.ap[:, :, 64:128], in_=bb[:, :, 64:128], func=AF.Copy), [bk], [KPo])
                bk, bb = tr4(vb)
                ACT(lambda e, bb=bb: e.activation(out=Vtok.ap.rearrange("p a b -> p (a b)"),
                                                  in_=bb.rearrange("p a b -> p (a b)"), func=AF.Copy), [bk], [Vtok])

                if DBG.get('stage', 99) < 5:
                    continue
                for h in range(8):
                    hp, e_ = h // 2, h % 2
                    AR, BT, KT = (ARe, BTe, KTe) if e_ == 0 else (ARo, BTo, KTo)
                    bk = nbank()
                    rhs = AR.ap[:, hp, :, :].rearrange("p a b -> p (a b)")
                    mm(bk.ap[:, 0:256], BT.ap[:, hp, :], rhs, True, True, [BT, AR], [bk])
                    mm(bk.ap[:, 256:512], KT.ap[:, hp, :], rhs, True, True, [KT, AR], [bk])
                    DVE(lambda e, bk=bk, h=h: e.tensor_tensor(
                        out=Amat.ap[:, h, :, :].rearrange("p (a b) c -> p a b c", a=2),
                        in0=bk.ap.rearrange("p (a b c) -> p a b c", a=2, b=2),
                        in1=msk.ap.rearrange("p (b c) -> p b c", b=2).unsqueeze(1).to_broadcast([128, 2, 2, 128]),
                        op=ALU.mult), [bk, msk], [View(None, ("reg", Amat.off + h * 1024, Amat.off + (h + 1) * 1024))])
                for half in range(2):
                    bk = nbank()
                    for q in range(4):
                        mm(bk.ap[:, q * 128:(q + 1) * 128], Amat.ap[:, half * 4 + q, 0, :], idb.ap, True, True,
                           [Amat, idb], [bk])
                    ACT(lambda e, bk=bk, half=half: e.activation(
                        out=X0.ap[:, half * 4:half * 4 + 4, :].rearrange("p a b -> p (a b)"), in_=bk.ap, func=AF.Copy),
                        [bk], [X0])
                if DBG.get('stage', 99) < 6:
                    continue
                def blkv(v, hb):
                    return vblk(v, hb * 4, hb * 4 + 4)

                XTping = [View(Amat.ap[:, hb * 4:hb * 4 + 4, 0, :], Amat.res) for hb in range(2)]
                ping = ([blkv(X0, 0), blkv(X0, 1)], XTping, [blkv(TTf, 0), blkv(TTf, 1)])
                pong = ([blkv(Xp, 0), blkv(Xp, 1)], [blkv(XTp, 0), blkv(XTp, 1)], [blkv(TTp, 0), blkv(TTp, 1)])
                for hb in range(2):
                    DVE(lambda e, hb=hb: e.tensor_tensor(
                        out=ping[2][hb].ap, in0=ping[1][hb].ap, in1=idb.ap.unsqueeze(1).to_broadcast([128, 4, 128]),
                        op=ALU.add), [ping[1][hb], idb], [ping[2][hb]])
                for lvl in range(6):
                    (Xc, XTc, TTc), (Xn, XTn, TTn) = (ping, pong) if lvl % 2 == 0 else (pong, ping)
                    bkx = [nbank(), nbank()]
                    for hb in range(2):
                        for i in range(4):
                            mm(bkx[hb].ap[:, i * 128:(i + 1) * 128], XTc[hb].ap[:, i, :], Xc[hb].ap[:, i, :], True, True,
                               [XTc[hb], Xc[hb]], [bkx[hb]])
                    if lvl < 5:
                        bkt = [nbank(), nbank()]
                        for hb in range(2):
                            for i in range(4):
                                mm(bkt[hb].ap[:, i * 128:(i + 1) * 128], Xc[hb].ap[:, i, :], XTc[hb].ap[:, i, :], True, True,
                                   [XTc[hb], Xc[hb]], [bkt[hb]])
                    for hb in range(2):
                        ACT(lambda e, hb=hb: e.activation(out=Xn[hb].ap, in_=bkx[hb].ap.rearrange("p (a b) -> p a b", a=4),
                                                          func=AF.Copy), [bkx[hb]], [Xn[hb]])
                    if lvl < 5:
                        for hb in range(2):
                            DVE(lambda e, hb=hb: e.tensor_scalar(out=XTn[hb].ap,
                                                                 in0=bkt[hb].ap.rearrange("p (a b) -> p a b", a=4),
                                                                 scalar1=1.0, scalar2=None, op0=ALU.mult), [bkt[hb]], [XTn[hb]])
                    bkT = [nbank(), nbank()]
                    for hb in range(2):
                        for i in range(4):
                            o = bkT[hb].ap[:, i * 128:(i + 1) * 128]
                            mm(o, Xn[hb].ap[:, i, :], TTc[hb].ap[:, i, :], True, True, [Xn[hb], TTc[hb]], [bkT[hb]])
                    for hb in range(2):
                        DVE(lambda e, hb=hb: e.tensor_tensor(out=TTn[hb].ap,
                                                             in0=bkT[hb].ap.rearrange("p (a b) -> p a b", a=4),
                                                             in1=TTc[hb].ap, op=ALU.add), [bkT[hb], TTc[hb]], [TTn[hb]])

                if DBG.get('stage', 99) < 7:
                    continue
                def pad(e_):
                    return (ARe, BPe, KPe) if e_ == 0 else (ARo, BPo, KPo)

                bk1 = nbank()
                for h in range(8):
                    hp, e_ = h // 2, h % 2
                    AR = pad(e_)[0]
                    o = bk1.ap[:, h * 64:(h + 1) * 64]
                    mm(o, AR.ap[:, hp, 0, :], STb.ap[:, hp, :], True, False, [AR, STb], [bk1])
                    mm(o, Amat.ap[:, h, 2, :], Vtok.ap[:, h, :], False, True, [Amat, Vtok], [bk1])
                ACT(lambda e: e.activation(out=X1b.ap.rearrange("p a b -> p (a b)"), in_=bk1.ap, func=AF.Copy), [bk1], [X1b])
                bk2 = nbank()
                for h in range(8):
                    mm(bk2.ap[:, h * 64:(h + 1) * 64], TTf.ap[:, h, :], X1b.ap[:, h, :], True, True, [TTf, X1b], [bk2])
                ACT(lambda e: e.activation(out=Ub.ap.rearrange("p a b -> p (a b)"), in_=bk2.ap, func=AF.Copy), [bk2], [Ub])
                bk3 = nbank()
                for h in range(8):
                    hp, e_ = h // 2, h % 2
                    AR = pad(e_)[0]
                    o = bk3.ap[:, h * 64:(h + 1) * 64]
                    mm(o, AR.ap[:, hp, 1, :], STb.ap[:, hp, :], True, False, [AR, STb], [bk3])
                    mm(o, Amat.ap[:, h, 1, :], Ub.ap[:, h, :], False, False, [Amat, Ub], [bk3])
                    mm(o, Amat.ap[:, h, 3, :], Vtok.ap[:, h, :], False, True, [Amat, Vtok], [bk3])
                bk4 = nbank()
                for hp in range(4):
                    o = bk4.ap[:, hp * 64:(hp + 1) * 64]
                    for e_ in range(2):
                        h = 2 * hp + e_
                        _, BPx, KPx = pad(e_)
                        mm(o, BPx.ap[:, hp, :], Ub.ap[:, h, :], e_ == 0, False, [BPx, Ub], [bk4])
                        mm(o, KPx.ap[:, hp, :], Vtok.ap[:, h, :], False, e_ == 1, [KPx, Vtok], [bk4])
                for hp in range(4):
                    DVE(lambda e, hp=hp: e.scalar_tensor_tensor(out=ST.ap[:, hp, :], in0=ST.ap[:, hp, :],
                                                                scalar=rW.ap[:, hp, 128:129],
                                                                in1=bk4.ap[:, hp * 64:(hp + 1) * 64],
                                                                op0=ALU.mult, op1=ALU.add), [ST, rW, bk4], [ST])
                ACT(lambda e: e.activation(out=STb.ap, in_=ST.ap, func=AF.Copy), [ST], [STb])

                if DBG.get('stage', 99) < 8:
                    continue
                ACT(lambda e: e.activation(out=Ytok.ap.rearrange("p a b -> p (a b)"), in_=bk3.ap, func=AF.Copy), [bk3], [Ytok])
                s1 = stats.ap[:, 0, :]
                s2 = stats.ap[:, 1, :]
                mean = stats.ap[:, 2, :]
                msq = stats.ap[:, 3, :]
                var = stats.ap[:, 4, :]
                rs = stats.ap[:, 5, :]
                DVE(lambda e: e.tensor_reduce(out=s1, in_=Ytok.ap, axis=AX.X, op=ALU.add), [Ytok], [stats])
                POOL(lambda e: e.tensor_tensor(out=Ysq.ap, in0=Ytok.ap, in1=Ytok.ap, op=ALU.mult), [Ytok], [Ysq])
                DVE(lambda e: e.tensor_reduce(out=s2, in_=Ysq.ap, axis=AX.X, op=ALU.add), [Ysq, stats], [stats])
                DVE(lambda e: e.tensor_scalar(out=mean, in0=s1, scalar1=1.0 / 64, scalar2=None, op0=ALU.mult), [stats], [stats])
                DVE(lambda e: e.tensor_tensor(out=msq, in0=mean, in1=mean, op=ALU.mult), [stats], [stats])
                DVE(lambda e: e.scalar_tensor_tensor(out=var, in0=s2, scalar=1.0 / 64, in1=msq, op0=ALU.mult,
                                                     op1=ALU.subtract), [stats], [stats])
                ACT(lambda e: e.activation(out=rs, in_=var, func=AF.Sqrt, bias=GN_EPS, scale=1.0), [stats], [stats])
                DVE(lambda e: e.reciprocal(out=rs, in_=rs), [stats], [stats])
                DVE(lambda e: e.tensor_tensor(out=Ysq.ap, in0=Ytok.ap, in1=mean.unsqueeze(2).to_broadcast([128, 8, 64]),
                                              op=ALU.subtract), [Ytok, stats, Ysq], [Ysq])
                DVE(lambda e: e.tensor_tensor(out=yn.ap, in0=Ysq.ap, in1=rs.unsqueeze(2).to_broadcast([128, 8, 64]),
                                              op=ALU.mult), [Ysq, stats], [yn])
                bk = nbank()
                bb = bk.ap
                ynf = yn.ap.rearrange("p a b -> p (a b)")
                for hp in range(4):
                    mm(bb[:, hp * 128:(hp + 1) * 128], ynf[:, hp * 128:(hp + 1) * 128], idb.ap, True, True, [yn, idb], [bk])
                for hp in range(4):
                    ACT(lambda e, hp=hp, bb=bb: e.activation(out=yg.ap[:, hp, :], in_=bb[:, hp * 128:(hp + 1) * 128],
                                                             func=AF.Identity, scale=pcol(l, 59 + hp),
                                                             bias=pcol(l, 63 + hp)), [bk, pv], [vs(yg, hp)])
                DVE(lambda e: e.tensor_tensor(out=yg.ap, in0=yg.ap, in1=bonus.ap, op=ALU.add), [yg, bonus], [yg])
                DVE(lambda e: e.tensor_tensor(out=cat.ap[:, 4:8, :], in0=yg.ap, in1=gT.ap, op=ALU.mult), [yg, gT], [cat])

                if DBG.get('stage', 99) < 9:
                    continue
                if j + 1 < NT:
                    rmsnorm([hT(j + 1)], slice((j + 1) * 128, (j + 2) * 128), 128, g1c + 0, sqv, rstd, hn,
                            lambda kc: vs(hn, kc))
                for half in range(2):
                    bk = nbank()
                    for q in range(4):
                        oc = half * 4 + q
                        for kc in range(KC):
                            mm(bk.ap[:, q * 128:(q + 1) * 128], wout.ap[:, kc, oc * 128:(oc + 1) * 128], cat.ap[:, kc, :],
                               kc == 0, kc == KC - 1, [wout, cat], [bk])
                    DVE(lambda e, bk=bk, half=half, cols=cols: e.tensor_tensor(
                        out=hT_t[:, half * 4:half * 4 + 4, cols], in0=hT_t[:, half * 4:half * 4 + 4, cols],
                        in1=bk.ap.rearrange("p (a b) -> p a b", a=4), op=ALU.add), [bk, hv], [hv])

            g2c = l * NCOL + 8
            for subs in (_ffn_tiling(T) if not DBG.get('noffn') else []):
                F = Bump(0, SCR_BYTES)
                mt0 = subs[0][0]
                ml = sum(n for _, n in subs)
                act = S.view(F.take([NF, 1032], BF16), [NF, 1032], BF16)
                hn2 = S.view(F.take([KC, 1032], BF16), [KC, 1032], BF16)
                sq2 = S.view(F.take([KC, 128], BF16), [KC, 128], BF16)
                rstd2 = S.view(F.take([128], F32), [128], F32)
                sil = [S.view(F.take([344], F32), [344], F32) for _ in range(2)]
                wg = [S.view(F.take([KC, 256], BF16), [KC, 256], BF16) for _ in range(2)]
                wu = [S.view(F.take([KC, 256], BF16), [KC, 256], BF16) for _ in range(2)]
                wd = [S.view(F.take([NF, 256], BF16), [NF, 256], BF16) for _ in range(2)]
                pc0 = mt0
                pi = 0
                while pc0 < mt0 + ml:
                    pn = min((pc0 // 128 + 1) * 128, mt0 + ml) - pc0
                    o = pc0 - mt0
                    sq_p = S.view(act.off + pi * 2560, [KC, 128], BF16)
                    rs_p = S.view(act.off + pi * 2560 + 2048, [128], F32)
                    rmsnorm(hT_range(pc0, pn), slice(pc0, pc0 + pn), pn, g2c, sq_p, rs_p, hn2,
                            lambda kc, o=o, pn=pn: vs(hn2, kc, o, o + pn))
                    pc0 += pn
                    pi += 1
                assert pi * 2560 <= NF * 1032 * 2
                for fp in range(NF // 2):
                    sl = fp % 2
                    DMA("pool", "wg%d" % sl, wg[sl].ap,
                        wgu_d[l, :, fp * 256:(fp + 1) * 256].rearrange("(k p) n -> p k n", p=128), [], [wg[sl]])
                    DMA("pool", "wu%d" % sl, wu[sl].ap,
                        wgu_d[l, :, D_FF + fp * 256:D_FF + (fp + 1) * 256].rearrange("(k p) n -> p k n", p=128),
                        [], [wu[sl]])
                    for fl in range(2):
                        f = fp * 2 + fl
                        for si, (s0, n) in enumerate(subs):
                            o = s0 - mt0
                            bg = nbank()
                            bu = nbank()
                            for kc in range(KC):
                                mm(bg.ap[:, 0:n], wg[sl].ap[:, kc, fl * 128:(fl + 1) * 128], hn2.ap[:, kc, o:o + n],
                                   kc == 0, kc == KC - 1, [wg[sl], hn2], [bg])
                            for kc in range(KC):
                                mm(bu.ap[:, 0:n], wu[sl].ap[:, kc, fl * 128:(fl + 1) * 128], hn2.ap[:, kc, o:o + n],
                                   kc == 0, kc == KC - 1, [wu[sl], hn2], [bu])
                            sv = sil[(f * 3 + si) % 2]
                            ACT(lambda e, bg=bg, sv=sv, n=n: e.activation(out=sv.ap[:, 0:n], in_=bg.ap[:, 0:n], func=AF.Silu),
                                [bg], [sv])
                            DVE(lambda e, bu=bu, sv=sv, n=n, f=f, o=o: e.tensor_tensor(
                                out=act.ap[:, f, o:o + n], in0=bu.ap[:, 0:n], in1=sv.ap[:, 0:n], op=ALU.mult),
                                [bu, sv], [vs(act, f, o, o + n)])
                for ocp in range(4):
                    sl = ocp % 2
                    DMA("pool", "wd%d" % sl, wd[sl].ap,
                        wdn_d[l, :, ocp * 256:(ocp + 1) * 256].rearrange("(f p) n -> p f n", p=128), [], [wd[sl]])
                    for ol in range(2):
                        oc = ocp * 2 + ol
                        for (s0, n) in subs:
                            o = s0 - mt0
                            bk = nbank()
                            for f in range(NF):
                                mm(bk.ap[:, 0:n], wd[sl].ap[:, f, ol * 128:(ol + 1) * 128], act.ap[:, f, o:o + n],
                                   f == 0, f == NF - 1, [wd[sl], vs(act, f, o, o + n)], [bk])
                            hvs = hT_range(s0, n)
                            DVE(lambda e, bk=bk, oc=oc, s0=s0, n=n: e.tensor_tensor(
                                out=hT_t[:, oc, s0:s0 + n], in0=hT_t[:, oc, s0:s0 + n], in1=bk.ap[:, 0:n], op=ALU.add),
                                [bk] + hvs, hvs)

        Fz = Bump(0, SCR_BYTES)
        sqf = S.view(Fz.take([KC, 128], BF16), [KC, 128], BF16)
        rstdf = S.view(Fz.take([128], F32), [128], F32)
        oT = S.view(Fz.take([KC, 128], F32), [KC, 128], F32)
        ost = [S.view(Fz.take([1024], F32), [1024], F32) for _ in range(2)]
        for j in range(NT):
            p0 = j * 128
            lo, hi = max(p0, N_META), min(p0 + 128, T)
            if hi <= lo:
                continue
            hv = hT(j)
            rmsnorm([hv], slice(p0, p0 + 128), 128, FG, sqf, rstdf, oT, lambda kc: vs(oT, kc))
            og = ost[j % 2]
            for half in range(2):
                bk = nbank()
                for q in range(4):
                    kc = half * 4 + q
                    tr(bk.ap[:, q * 128:(q + 1) * 128], oT.ap[:, kc, :], idf, [oT], [bk])
                ACT(lambda e, bk=bk, og=og, half=half: e.activation(out=og.ap[:, half * 512:(half + 1) * 512],
                                                                    in_=bk.ap, func=AF.Copy), [bk], [og])
            ev = DMA("sp", "ost%d" % (j % 2), out_d[sq_i, lo - N_META:hi - N_META, :], og.ap[lo - p0:hi - p0, :], [og], [])
            P.final_events.append(ev)

    sem_names = list(ENGS[:4]) + sorted(set("dma:" + k for k in P.dma_cnt))
    sems = {}
    for i, nm in enumerate(sem_names):
        sems[nm] = es.enter_context(nc.semaphore("s%d" % i))
    fin = {}
    for (k, v) in P.final_events:
        fin[k] = max(fin.get(k, 0), v)
    block = es.enter_context(nc.Block())

    def run_stream(engname, e, extra_waits=()):
        for (waits, f, ev) in P.streams[engname]:
            for (k, val) in waits:
                e.wait_ge(sems[k], val)
            ins = getattr(e, f[0])(*f[1], **f[2])
            if ev[0].startswith("dma:"):
                ins.then_inc(sems[ev[0]], 16)
            else:
                ins.then_inc(sems[ev[0]], 1)
        for (k, val) in extra_waits:
            e.wait_ge(sems[k], val)

    @block.sync
    def _(e):
        run_stream("sp", e, list(fin.items()))

    @block.scalar
    def _(e):
        run_stream("act", e)

    @block.vector
    def _(e):
        run_stream("dve", e)

    @block.gpsimd
    def _(e):
        run_stream("pool", e)

    @block.tensor
    def _(e):
        run_stream("pe", e)

    es.close()
    return nc, {e: len(P.streams[e]) for e in ENGS}


def _consts():
    bf = ml_dtypes.bfloat16
    s = np.arange(128)[:, None]
    t = np.arange(128)[None, :]
    strict = (s < t).astype(np.float32)
    incl = (s <= t).astype(np.float32)
    blk = ((s // 64) == (t // 64)).astype(np.float32)
    rc0 = np.zeros((128, 4, 16), np.float32)
    for g, win in enumerate((2, 4, 8, 16)):
        rc0[:, g, :] = 1.0 / np.minimum(np.arange(16) + 1, win)[None, :]
    return {
        "c_ident_f": np.eye(128, dtype=np.float32),
        "c_ones_f": np.ones((128, 128), np.float32),
        "c_rc0": rc0.reshape(128, 64),
        "c_ident_b": np.eye(128, dtype=np.float32).astype(bf),
        "c_mask": np.concatenate([strict, incl], 1).astype(bf),
        "c_blk": blk.astype(bf),
        "c_o1k": np.full((128, 128), 1.0 / 1024, np.float32).astype(bf),
    }


def _pvec(inp, DEPTH):
    pv = np.zeros((128, DEPTH * NCOL + 8), np.float32)

    def put(col, vec, n):
        pv[:, col:col + n] = np.asarray(vec, np.float32).reshape(n, 128).T

    for l in range(DEPTH):
        b = l * NCOL
        put(b + 0, inp["ln1_g"][l], 8)
        put(b + 8, inp["ln2_g"][l], 8)
        mu = np.asarray(inp["mu_shift"][l], np.float32)
        put(b + 16, mu[0:1792], 14)
        pv[0:32, b + 30] = mu[1792:1824]
        if l >= 1:
            pv[32:64, b + 30] = np.asarray(inp["mu_vres"][l - 1], np.float32)
        put(b + 31, inp["pool_scale"][l], 4)
        put(b + 35, inp["w0"][l], 4)
        put(b + 39, inp["a0"][l], 4)
        put(b + 43, inp["k_k"][l], 4)
        put(b + 47, inp["k_a"][l], 4)
        put(b + 51, inp["r_k"][l], 4)
        if l >= 1:
            put(b + 55, inp["v0"][l - 1], 4)
        put(b + 59, inp["gn_w"][l], 4)
        put(b + 63, inp["gn_b"][l], 4)
    put(DEPTH * NCOL, inp["final_g"], 8)
    return pv


_CACHE = {}


def run(inputs, DEPTH, SEQ, BATCH, n_cores):
    NSEQ = BATCH // n_cores
    key = (DEPTH, SEQ, NSEQ)
    if key not in _CACHE:
        _CACHE[key] = build_program(DEPTH, SEQ, NSEQ)
    nc, counts = _CACHE[key]
    f32 = lambda a: np.ascontiguousarray(np.asarray(a, dtype=np.float32))
    shared = {k: f32(inputs[k]) for k in ("meta_tokens", "w_in", "pool_w", "w_lora_up", "a_lora_up", "g_lora_up",
                                          "w_out", "w_gate_up", "w_down")}
    if DEPTH > 1:
        shared["w_in_vres"] = f32(inputs["w_in_vres"])
        shared["v_lora_up"] = f32(inputs["v_lora_up"])
    else:
        shared["w_in_vres"] = np.zeros((1, D, 32), np.float32)
        shared["v_lora_up"] = np.zeros((1, 32, RW), np.float32)
    shared["pvec"] = _pvec(inputs, DEPTH)
    shared.update(_consts())
    x = f32(inputs["x"])
    in_maps = []
    for c in range(n_cores):
        m = dict(shared)
        m["x"] = np.ascontiguousarray(x[c * NSEQ:(c + 1) * NSEQ])
        in_maps.append(m)
    if DBG.get("trace"):
        res = run_bass_kernel_spmd(nc, in_maps, core_ids=list(range(n_cores)), trace=True)
        print("EXEC_TIME_NS", res.exec_time_ns, flush=True)
        DBG["res"] = res
    else:
        res = run_bass_kernel_spmd(nc, in_maps, core_ids=list(range(n_cores)))
    return np.concatenate([np.asarray(r["out"], dtype=np.float32) for r in res.results], axis=0)


def kernel(**inputs):
    return run(inputs, 4, 2048, 16, 8)
```
